# Optimizing a Trainium2 kernel written in Bass

```python
import math
import jax, jax.numpy as jnp
from jax import lax
import numpy as np

D_MODEL = 4096
BATCH = 8
SEQ = 2048
DEPTH = 2
DEC_BATCH = 8
DEC_SEQ = 64
PAST_LEN = 2048

CHUNK = 64
Q_BLOCK = 128
D_HEAD = 128
MIX_A = D_MODEL // 2
MIX_B = D_MODEL - MIX_A
H_A = MIX_A // (2 * D_HEAD)
H_B = MIX_B // D_HEAD
QKV_WIDTH = 3 * MIX_A + 3 * MIX_B
D_FF = 4 * D_MODEL
CONV_WIDTH = 31
ROPE_THETA = 10000.0
NORM_EPS = 1e-6
SUBLN_EPS = 1e-5
N_ATTN = (DEPTH + 1) // 2
N_CONV = DEPTH // 2
NEG_INF = -1e30

kernel_name = 'hybrid_stream_diffattn_stickbreak_conformer_step'


def rmsnorm(x, g, eps=NORM_EPS):
    xf = x.astype(jnp.float32)
    y = xf * lax.rsqrt(jnp.mean(xf * xf, axis=-1, keepdims=True) + eps)
    return (y * g.astype(jnp.float32)).astype(x.dtype)


def layernorm(x, g, b, eps=NORM_EPS):
    xf = x.astype(jnp.float32)
    xc = xf - jnp.mean(xf, axis=-1, keepdims=True)
    var = jnp.mean(xc * xc, axis=-1, keepdims=True)
    return (xc * lax.rsqrt(var + eps) * g.astype(jnp.float32) + b.astype(jnp.float32)).astype(x.dtype)


def rope(x, pos):
    half = D_HEAD // 2
    inv_freq = jnp.power(ROPE_THETA, -jnp.arange(half, dtype=jnp.float32) / half)
    ang = pos.astype(jnp.float32)[:, None] * inv_freq[None, :]
    cos = jnp.cos(ang)[None, :, None, :]
    sin = jnp.sin(ang)[None, :, None, :]
    xf = x.astype(jnp.float32)
    x1, x2 = xf[..., :half], xf[..., half:]
    return jnp.concatenate([x1 * cos - x2 * sin, x2 * cos + x1 * sin], axis=-1).astype(x.dtype)


def sweep_query_blocks(fn, q, q_pos):
    B, S = q.shape[0], q.shape[1]
    nb = S // Q_BLOCK
    qb = jnp.moveaxis(q.reshape((B, nb, Q_BLOCK) + q.shape[2:]), 1, 0)
    pb = q_pos.reshape(nb, Q_BLOCK)
    out = lax.map(lambda args: fn(args[0], args[1]), (qb, pb))
    return jnp.moveaxis(out, 0, 1).reshape(B, S, -1)


def diff_attn_block(q, k, v, q_pos, k_pos, lam, lam_init, subln_g):
    B, Tq = q.shape[0], q.shape[1]
    Tk = k.shape[1]
    s = jnp.einsum('bqhd,bkhd->bhqk', q, k, preferred_element_type=jnp.float32) / math.sqrt(D_HEAD)
    visible = (k_pos[None, :] // CHUNK) <= (q_pos[:, None] // CHUNK)
    s = jnp.where(visible[None, None], s, NEG_INF)
    p = jax.nn.softmax(s, axis=-1).reshape(B, H_A, 2, Tq, Tk)
    w = p[:, :, 0] - lam * p[:, :, 1]
    o = jnp.einsum('bhqk,bkhe->bqhe', w.astype(v.dtype), v)
    o = rmsnorm(o, subln_g, SUBLN_EPS) * (1.0 - lam_init)
    return o.reshape(B, Tq, MIX_A)


def stick_breaking_block(q, k, v, q_pos, k_pos):
    B, Tq = q.shape[0], q.shape[1]
    z = jnp.einsum('bqhd,bkhd->bhqk', q, k, preferred_element_type=jnp.float32) / math.sqrt(D_HEAD)
    before = (k_pos[None, :] < q_pos[:, None])[None, None]
    log_beta = jax.nn.log_sigmoid(z)
    log_rest = jnp.where(before, jax.nn.log_sigmoid(-z), 0.0)
    rev = lax.cumsum(log_rest, axis=3, reverse=True)
    after = jnp.concatenate([rev[..., 1:], jnp.zeros_like(rev[..., :1])], axis=-1)
    a = jnp.where(before, jnp.exp(log_beta + after), 0.0)
    o = jnp.einsum('bhqk,bkhd->bqhd', a.astype(v.dtype), v)
    return o.reshape(B, Tq, MIX_B)


def parallel_attn_mixer(h, pos, cache, w_in, w_out, lq1, lk1, lq2, lk2, subln_g, layer):
    B, T = h.shape[0], h.shape[1]
    proj = h @ w_in
    qa, ka, va, qb, kb, vb = jnp.split(
        proj, [MIX_A, 2 * MIX_A, 3 * MIX_A, 3 * MIX_A + MIX_B, 3 * MIX_A + 2 * MIX_B], axis=-1)
    qa = rope(qa.reshape(B, T, 2 * H_A, D_HEAD), pos)
    ka = rope(ka.reshape(B, T, 2 * H_A, D_HEAD), pos)
    va = va.reshape(B, T, H_A, 2 * D_HEAD)
    qb = qb.reshape(B, T, H_B, D_HEAD)
    kb = kb.reshape(B, T, H_B, D_HEAD)
    vb = vb.reshape(B, T, H_B, D_HEAD)
    lam_init = 0.8 - 0.6 * math.exp(-0.3 * layer)
    lam = (jnp.exp(jnp.sum(lq1.astype(jnp.float32) * lk1.astype(jnp.float32)))
           - jnp.exp(jnp.sum(lq2.astype(jnp.float32) * lk2.astype(jnp.float32))) + lam_init)
    if cache is None:
        ka_all, va_all, kb_all, vb_all, k_pos = ka, va, kb, vb, pos
    else:
        ck_a, cv_a, ck_b, cv_b = cache
        past_len = ck_a.shape[1]
        ka_all = jnp.concatenate([ck_a, ka], axis=1)
        va_all = jnp.concatenate([cv_a, va], axis=1)
        kb_all = jnp.concatenate([ck_b, kb], axis=1)
        vb_all = jnp.concatenate([cv_b, vb], axis=1)
        k_pos = jnp.concatenate([jnp.arange(past_len), pos])
    fa = lambda qblk, pblk: diff_attn_block(qblk, ka_all, va_all, pblk, k_pos, lam, lam_init, subln_g)
    fb = lambda qblk, pblk: stick_breaking_block(qblk, kb_all, vb_all, pblk, k_pos)
    if cache is None:
        o_a = sweep_query_blocks(fa, qa, pos)
        o_b = sweep_query_blocks(fb, qb, pos)
    else:
        o_a = fa(qa, pos)
        o_b = fb(qb, pos)
    out = jnp.concatenate([o_a, o_b], axis=-1) @ w_out
    return out, (ka, va, kb, vb)


def conv_module(h, state, pw1, dw, dw_b, ln_g, ln_b, pw2):
    ab = h @ pw1
    u = ab[..., :D_MODEL] * jax.nn.sigmoid(ab[..., D_MODEL:])
    if state is None:
        state = jnp.zeros((u.shape[0], CONV_WIDTH - 1, D_MODEL), u.dtype)
    up = jnp.concatenate([state.astype(u.dtype), u], axis=1)
    y = lax.conv_general_dilated(up, dw[:, None, :].astype(u.dtype), (1,), 'VALID',
                                 dimension_numbers=('NWC', 'WIO', 'NWC'),
                                 feature_group_count=D_MODEL) + dw_b
    y = jax.nn.silu(layernorm(y, ln_g, ln_b))
    return y @ pw2, up[:, -(CONV_WIDTH - 1):]


def squared_relu_mlp(h, up, down):
    return jnp.square(jax.nn.relu(h @ up)) @ down


def run_trunk(x, c, past, p):
    B, T = x.shape[0], x.shape[1]
    if past is None:
        pos = jnp.arange(T)
    else:
        pos = past['attn'][0].shape[2] + jnp.arange(T)
    new_attn, new_conv = [], []
    cs = jax.nn.silu(c)
    for layer in range(DEPTH):
        mod = cs @ p['w_mod'][layer] + p['b_mod'][layer]
        sh_m, sc_m, g_m, sh_f, sc_f, g_f = [m[:, None, :] for m in jnp.split(mod, 6, axis=-1)]
        h = rmsnorm(x, p['norm_mix'][layer]) * (1.0 + sc_m) + sh_m
        i = layer // 2
        if layer % 2 == 0:
            cache = None if past is None else tuple(a[i] for a in past['attn'])
            out, rows = parallel_attn_mixer(h, pos, cache, p['w_attn_in'][i], p['w_attn_out'][i],
                                            p['lambda_q1'][i], p['lambda_k1'][i], p['lambda_q2'][i],
                                            p['lambda_k2'][i], p['diff_subln_g'][i], layer)
            new_attn.append(rows)
        else:
            st = None if past is None else past['conv'][i]
            out, st_new = conv_module(h, st, p['conv_pw1'][i], p['conv_dw'][i], p['conv_dw_b'][i],
                                      p['conv_ln_g'][i], p['conv_ln_b'][i], p['conv_pw2'][i])
            new_conv.append(st_new)
        x = x + g_m * out
        h = rmsnorm(x, p['norm_mlp'][layer]) * (1.0 + sc_f) + sh_f
        x = x + g_f * squared_relu_mlp(h, p['mlp_up'][layer], p['mlp_down'][layer])
    y = rmsnorm(x, p['final_g'])
    k_diff = jnp.stack([r[0] for r in new_attn])
    v_diff = jnp.stack([r[1] for r in new_attn])
    k_sb = jnp.stack([r[2] for r in new_attn])
    v_sb = jnp.stack([r[3] for r in new_attn])
    conv_state = jnp.stack(new_conv)
    return y, k_diff, v_diff, k_sb, v_sb, conv_state


def setup_inputs(seed: int = 0) -> dict:
    key = jax.random.key(seed)
    ks = jax.random.split(key, 32)
    f32 = jnp.float32
    def nrm(i, shape, scale):
        return jax.random.normal(ks[i], shape, f32) * scale
    d = D_MODEL
    return {
        'x_prompt': nrm(0, (BATCH, SEQ, d), 1.0),
        'x_sample': nrm(1, (DEC_BATCH, DEC_SEQ, d), 1.0),
        'c_prompt': nrm(2, (BATCH, d), 1.0),
        'c_sample': nrm(3, (DEC_BATCH, d), 1.0),
        'cache_k_diff': nrm(4, (N_ATTN, DEC_BATCH, PAST_LEN, 2 * H_A, D_HEAD), 1.0),
        'cache_v_diff': nrm(5, (N_ATTN, DEC_BATCH, PAST_LEN, H_A, 2 * D_HEAD), 1.0),
        'cache_k_sb': nrm(6, (N_ATTN, DEC_BATCH, PAST_LEN, H_B, D_HEAD), 1.0),
        'cache_v_sb': nrm(7, (N_ATTN, DEC_BATCH, PAST_LEN, H_B, D_HEAD), 1.0),
        'state_conv': nrm(8, (N_CONV, DEC_BATCH, CONV_WIDTH - 1, d), 0.5),
        'w_mod': nrm(9, (DEPTH, d, 6 * d), 0.5 * d ** -0.5),
        'b_mod': nrm(10, (DEPTH, 6 * d), 0.02),
        'norm_mix': 1.0 + nrm(11, (DEPTH, d), 0.02),
        'norm_mlp': 1.0 + nrm(12, (DEPTH, d), 0.02),
        'w_attn_in': nrm(13, (N_ATTN, d, QKV_WIDTH), d ** -0.5),
        'w_attn_out': nrm(14, (N_ATTN, MIX_A + MIX_B, d), (MIX_A + MIX_B) ** -0.5),
        'lambda_q1': nrm(15, (N_ATTN, D_HEAD), 0.1),
        'lambda_k1': nrm(16, (N_ATTN, D_HEAD), 0.1),
        'lambda_q2': nrm(17, (N_ATTN, D_HEAD), 0.1),
        'lambda_k2': nrm(18, (N_ATTN, D_HEAD), 0.1),
        'diff_subln_g': 1.0 + nrm(19, (N_ATTN, 2 * D_HEAD), 0.02),
        'conv_pw1': nrm(20, (N_CONV, d, 2 * d), d ** -0.5),
        'conv_dw': nrm(21, (N_CONV, CONV_WIDTH, d), CONV_WIDTH ** -0.5),
        'conv_dw_b': nrm(22, (N_CONV, d), 0.02),
        'conv_ln_g': 1.0 + nrm(23, (N_CONV, d), 0.02),
        'conv_ln_b': nrm(24, (N_CONV, d), 0.02),
        'conv_pw2': nrm(25, (N_CONV, d, d), d ** -0.5),
        'mlp_up': nrm(26, (DEPTH, d, D_FF), d ** -0.5),
        'mlp_down': nrm(27, (DEPTH, D_FF, d), D_FF ** -0.5),
        'final_g': 1.0 + nrm(28, (d,), 0.02),
    }


def reference(x_prompt, x_sample, c_prompt, c_sample, cache_k_diff, cache_v_diff, cache_k_sb, cache_v_sb,
              state_conv, w_mod, b_mod, norm_mix, norm_mlp, w_attn_in, w_attn_out, lambda_q1, lambda_k1,
              lambda_q2, lambda_k2, diff_subln_g, conv_pw1, conv_dw, conv_dw_b, conv_ln_g, conv_ln_b, conv_pw2,
              mlp_up, mlp_down, final_g):
    p = dict(w_mod=w_mod, b_mod=b_mod, norm_mix=norm_mix, norm_mlp=norm_mlp, w_attn_in=w_attn_in,
             w_attn_out=w_attn_out, lambda_q1=lambda_q1, lambda_k1=lambda_k1, lambda_q2=lambda_q2,
             lambda_k2=lambda_k2, diff_subln_g=diff_subln_g, conv_pw1=conv_pw1, conv_dw=conv_dw,
             conv_dw_b=conv_dw_b, conv_ln_g=conv_ln_g, conv_ln_b=conv_ln_b, conv_pw2=conv_pw2,
             mlp_up=mlp_up, mlp_down=mlp_down, final_g=final_g)
    y_prompt, kd_p, vd_p, ks_p, vs_p, conv_p = run_trunk(x_prompt, c_prompt, None, p)
    past = dict(attn=(cache_k_diff, cache_v_diff, cache_k_sb, cache_v_sb), conv=state_conv)
    y_sample, kd_s, vd_s, ks_s, vs_s, conv_s = run_trunk(x_sample, c_sample, past, p)
    return (y_prompt, y_sample, kd_p, vd_p, ks_p, vs_p, conv_p, kd_s, vd_s, ks_s, vs_s, conv_s)
```

```python
import math
from contextlib import ExitStack

import numpy as np
import concourse.bass as bass
import concourse.mybir as mybir
from concourse.bass_utils import run_bass_kernel_spmd

F32 = mybir.dt.float32
BF16 = mybir.dt.bfloat16
AF = mybir.ActivationFunctionType
ALU = mybir.AluOpType
AX = mybir.AxisListType

FULL = dict(D=4096, SEQ=2048, DEC=64, PAST=2048, CH=512)
NORM_EPS = 1e-6
SUBLN_EPS = 1e-5
CW = 31


class Sched:
    ENG = ("pe", "act", "dve", "pool", "sp")

    def __init__(self, nc):
        self.nc = nc
        self.ops = []
        self.last_w = {}
        self.readers = {}

    dry = False

    def op(self, eng, fn, reads=(), writes=(), slot=None):
        if self.dry:
            return -1
        deps = set()
        for r in reads:
            if r in self.last_w:
                deps.add(self.last_w[r])
        for w in writes:
            if w in self.last_w:
                deps.add(self.last_w[w])
            for rd in self.readers.get(w, ()):
                deps.add(rd)
        idx = len(self.ops)
        self.ops.append(dict(eng=eng, fn=fn, deps=deps, slot=slot, used=False))
        for d in deps:
            self.ops[d]["used"] = True
        for w in writes:
            self.last_w[w] = idx
            self.readers[w] = []
        for r in reads:
            if r not in writes:
                self.readers.setdefault(r, []).append(idx)
        return idx

    def dma(self, eng, fn, slot, reads=(), writes=()):
        return self.op(eng, fn, reads, writes, slot=slot)

    _n = [0]

    pool_stack = None
    esem = {}
    ssem = {}
    ecount = {}
    scount = {}

    @classmethod
    def reset_pool(cls, stack):
        cls.pool_stack = stack
        cls.esem = {}
        cls.ssem = {}
        cls.ecount = {}
        cls.scount = {}

    def emit(self, name="blk"):
        nc = self.nc
        ops = self.ops
        if not ops:
            return
        cls = Sched
        for e in self.ENG:
            if e not in cls.esem:
                cls.esem[e] = cls.pool_stack.enter_context(nc.semaphore(f"sem_e_{e}"))
                cls.ecount[e] = 0
        for o in ops:
            sl = o["slot"]
            if sl is not None and sl not in cls.ssem:
                cls.ssem[sl] = cls.pool_stack.enter_context(nc.semaphore(f"sem_s_{sl}"))
                cls.scount[sl] = 0
        esem, ssem, ecount, scount = cls.esem, cls.ssem, cls.ecount, cls.scount
        with ExitStack() as st:
            block = st.enter_context(nc.Block())
            handles = {}

            def run_engine(engname, e):
                waited = {}
                mine = [i for i, o in enumerate(ops) if o["eng"] == engname]
                last_dma = {}
                for i in mine:
                    o = ops[i]
                    for d in sorted(o["deps"]):
                        p = ops[d]
                        if p["eng"] == engname and p["slot"] is None and engname == "pe":
                            continue
                        sem, val = p["sig"]
                        if waited.get(id(sem), 0) >= val:
                            continue
                        e.wait_ge(sem, val)
                        waited[id(sem)] = val
                    res = o["fn"](e)
                    if o["slot"] is not None:
                        sem, val = o["sig"]
                        lst = res if isinstance(res, (list, tuple)) else [res]
                        for ins in lst:
                            ins.then_inc(sem, 16)
                        last_dma[o["slot"]] = (sem, val)
                    else:
                        sem, val = o["sig"]
                        res.then_inc(sem, 1)
                for s, (sem, val) in last_dma.items():
                    if waited.get(id(sem), 0) < val:
                        e.wait_ge(sem, val)

            for o in ops:
                if o["slot"] is not None:
                    n = getattr(o["fn"], "ndma", 1)
                    scount[o["slot"]] += 16 * n
                    o["sig"] = (ssem[o["slot"]], scount[o["slot"]])
                else:
                    ecount[o["eng"]] += 1
                    o["sig"] = (esem[o["eng"]], ecount[o["eng"]])

            used = {o["eng"] for o in ops}
            if "pe" in used:
                @block.tensor
                def _(e):
                    run_engine("pe", e)
            if "act" in used:
                @block.scalar
                def _(e):
                    run_engine("act", e)
            if "dve" in used:
                @block.vector
                def _(e):
                    run_engine("dve", e)
            if "pool" in used:
                @block.gpsimd
                def _(e):
                    run_engine("pool", e)
            if "sp" in used:
                @block.sync
                def _(e):
                    run_engine("sp", e)
        self.ops = []
        self.last_w = {}
        self.readers = {}


def dmafn(fn, n):
    fn.ndma = n
    return fn


def build(cfg, debug_outs=()):
    D, SEQ, DEC, PAST, CH = cfg["D"], cfg["SEQ"], cfg["DEC"], cfg["PAST"], cfg["CH"]
    KC = D // 128
    KS = min(8, KC)
    HA = D // 512
    NQA = 2 * HA
    HB = D // 256
    DH = D // 2
    QKV = 3 * D
    DFF = 4 * D
    TT = SEQ + DEC
    BWQ = min(512, DH)
    NPT = SEQ // 128
    lam_init = 0.8 - 0.6 * math.exp(-0.3 * 0)

    nc = bass.Bass("TRN2", target_bir_lowering=False)
    uid = [0]

    def SBT(name, shape, dt):
        uid[0] += 1
        return nc.sbuf_tensor(f"{name}_{uid[0]}", shape, dt)

    def PST(name, shape, dt):
        uid[0] += 1
        return nc.psum_tensor(f"{name}_{uid[0]}", shape, dt)

    def din(name, shape, dt=F32):
        return nc.dram_tensor(name, list(shape), dt, kind="ExternalInput").ap()

    def dout(name, shape, dt=F32):
        return nc.dram_tensor(name, list(shape), dt, kind="ExternalOutput").ap()

    def dscr(name, shape, dt=F32):
        kind = "ExternalOutput" if name in debug_outs else "Internal"
        return nc.dram_tensor(name, list(shape), dt, kind=kind).ap()

    x_p = din("x_p", [SEQ, D]); x_s = din("x_s", [DEC, D])
    c_p = din("c_p", [KC, 128]); c_s = din("c_s", [KC, 128])
    ck_d = din("ck_d", [PAST, NQA * 128]); cv_d = din("cv_d", [PAST, DH])
    ck_s = din("ck_s", [PAST, DH]); cv_s = din("cv_s", [PAST, DH])
    st_c = din("st_c", [CW - 1, D])
    w_mod = din("w_mod", [2, D, 6 * D]); b_mod = din("b_mod", [2, 6 * KC, 128])
    n_mix = din("n_mix", [2, KC, 128]); n_mlp = din("n_mlp", [2, KC, 128])
    w_in = din("w_in", [D, QKV]); w_out = din("w_out", [D, D])
    lam_v = din("lam_v", [4, 128]); subln = din("subln", [1, 256])
    pw1 = din("pw1", [D, 2 * D]); cdw = din("cdw", [CW, D])
    cdw_b = din("cdw_b", [KC, 128]); cln_g = din("cln_g", [KC, 128]); cln_b = din("cln_b", [KC, 128])
    pw2 = din("pw2", [D, D])
    m_up = din("m_up", [2, D, DFF]); m_dn = din("m_dn", [2, DFF, D])
    fin_g = din("fin_g", [KC, 128])
    k_id = din("k_id", [128, 128]); k_ropeC = din("k_ropeC", [TT, 128]); k_ropeS = din("k_ropeS", [TT, 128])
    k_mdiff = din("k_mdiff", [128, 128]); k_msb01 = din("k_msb01", [128, 128]); k_msbneg = din("k_msbneg", [128, 128])

    y_p = dout("y_p", [SEQ, D]); y_s = dout("y_s", [DEC, D])
    kd_p = dout("kd_p", [SEQ, DH]); vd_p = dout("vd_p", [SEQ, DH]); ks_p = dout("ks_p", [SEQ, DH]); vs_p = dout("vs_p", [SEQ, DH])
    cv_p = dout("cv_p", [CW - 1, D])
    kd_s = dout("kd_s", [DEC, DH]); vd_s = dout("vd_s", [DEC, DH]); ks_s = dout("ks_s", [DEC, DH]); vs_s = dout("vs_s", [DEC, DH])
    cv_so = dout("cv_s_o", [CW - 1, D])

    xT = dscr("xT", [D, TT])
    qT = dscr("qT", [NQA + HB, 128, TT], BF16)
    kT = dscr("kT", [NQA + HB, 128, TT], BF16)
    vbf = dscr("vbf", [TT, D], BF16)
    gT = dscr("gT", [DFF, TT], BF16)
    uTp = dscr("uTp", [D, CW - 1 + SEQ])
    uTs = dscr("uTs", [D, CW - 1 + DEC])
    yT = dscr("yT", [D, TT])
    xTv = xT.rearrange("(k p) t -> p k t", p=128)
    gTv = gT.rearrange("(k p) t -> p k t", p=128)
    uTpv = uTp.rearrange("(k p) t -> p k t", p=128)
    uTsv = uTs.rearrange("(k p) t -> p k t", p=128)
    yTv = yT.rearrange("(k p) t -> p k t", p=128)

    groups = []
    npc = SEQ // CH
    half = max(1, npc // 2)
    g0 = [("P", i * CH, CH, i * CH) for i in range(half)] + [("S", 0, DEC, SEQ)]
    g1 = [("P", i * CH, CH, i * CH) for i in range(half, npc)]
    groups = [g0] + ([g1] if g1 else [])
    GMAX = max(sum(c[2] for c in g) for g in groups)

    def tiles_of(group):
        out = []
        col = 0
        for (kind, pos0, n, tok0) in group:
            for o in range(0, n, 128):
                m = min(128, n - o)
                out.append((kind, pos0 + o, m, tok0 + o, col + o))
            col += n
        return out

    def chunks_of(group):
        out = []
        col = 0
        for (kind, pos0, n, tok0) in group:
            out.append((kind, pos0, n, tok0, col))
            col += n
        return out

    es = ExitStack()
    with es:
        Sched.reset_pool(es)

        def sb(name, shape, dt=F32):
            return es.enter_context(SBT(name, list(shape), dt))

        Abuf = sb("Abuf", [128, KC, GMAX], BF16)
        Wb = [None, None]

        def alloc_W(ps):
            for i in range(2):
                Wb[i] = ps.enter_context(SBT(f"Wb{i}", [128, KC, 512], BF16))
        ident = sb("ident", [128, 128]); identb = sb("identb", [128, 128], BF16)
        onesf = sb("onesf", [128, 128])
        mdiff = sb("mdiff", [128, 128]); msb01 = sb("msb01", [128, 128]); msbneg = sb("msbneg", [128, 128])
        modT = sb("modT", [128, 2, 6, KC, 2])
        bmodT = sb("bmodT", [128, 2, 6 * KC])
        nmixT = sb("nmixT", [128, 2, KC]); nmlpT = sb("nmlpT", [128, 2, KC]); fingT = sb("fingT", [128, KC])
        dwT = sb("dwT", [128, KC, CW]); dwbT = sb("dwbT", [128, KC]); clngT = sb("clngT", [128, KC]); clnbT = sb("clnbT", [128, KC])
        csT = sb("csT", [128, KC, 2], BF16)
        sgbc = sb("sgbc", [128, 256])
        lamt = sb("lamt", [128, 4])
        PA = sb("PA", [128, 2, 2, 2, KC]); PB = sb("PB", [128, 2, 2, 2, KC]); PG = sb("PG", [128, 2, 2, 2, KC])

        KI = {"P": 0, "S": 1}

        def phase_consts():
            S = Sched(nc)
            with ExitStack() as ps:
                def t(name, shape, dt=F32):
                    return ps.enter_context(SBT(name, list(shape), dt))
                pt = ps.enter_context(PST("c_pt", [128, 512], F32))
                S.dma("sp", dmafn(lambda e: e.dma_start(out=ident[:], in_=k_id[:, :]), 1), "ld_id", writes=["ident"])
                S.dma("sp", dmafn(lambda e: e.dma_start(out=mdiff[:], in_=k_mdiff[:, :]), 1), "ld_m1", writes=["mdiff"])
                S.dma("sp", dmafn(lambda e: e.dma_start(out=msb01[:], in_=k_msb01[:, :]), 1), "ld_m2", writes=["msb01"])
                S.dma("sp", dmafn(lambda e: e.dma_start(out=msbneg[:], in_=k_msbneg[:, :]), 1), "ld_m3", writes=["msbneg"])
                S.op("dve", lambda e: e.tensor_copy(out=identb[:], in_=ident[:]), reads=["ident"], writes=["identb"])
                S.op("dve", lambda e: e.memset(onesf[:], 1.0), writes=["onesf"])
                S.dma("sp", dmafn(lambda e: e.dma_start(out=sgbc[:], in_=subln[0:1, :].broadcast_to([128, 256])), 1), "ld_sg", writes=["sgbc"])
                S.op("dve", lambda e: e.tensor_scalar(out=sgbc[:], in0=sgbc[:], scalar1=float(1.0 - lam_init), scalar2=None, op0=ALU.mult),
                     reads=["sgbc"], writes=["sgbc"])
                lv = t("lv", [128, 4, 128])
                S.dma("sp", dmafn(lambda e: e.dma_start(out=lv[:], in_=lam_v.rearrange("(o a) b -> o a b", o=1).broadcast_to([128, 4, 128])), 1), "ld_lv", writes=["lv"])
                lp = t("lp", [128, 2, 128]); ls = t("ls", [128, 2]); le = t("le", [128, 2])
                S.op("dve", lambda e: e.tensor_tensor(out=lp[:, 0, :], in0=lv[:, 0, :], in1=lv[:, 1, :], op=ALU.mult), reads=["lv"], writes=["lp0"])
                S.op("dve", lambda e: e.tensor_tensor(out=lp[:, 1, :], in0=lv[:, 2, :], in1=lv[:, 3, :], op=ALU.mult), reads=["lv"], writes=["lp1"])
                S.op("dve", lambda e: e.tensor_reduce(out=ls[:], in_=lp[:], axis=AX.X, op=ALU.add), reads=["lp0", "lp1"], writes=["ls"])
                S.op("act", lambda e: e.activation(out=le[:], in_=ls[:], func=AF.Exp), reads=["ls"], writes=["le"])
                S.op("dve", lambda e: e.tensor_tensor(out=lamt[:, 0:1], in0=le[:, 1:2], in1=le[:, 0:1], op=ALU.subtract), reads=["le"], writes=["lamt"])
                S.op("dve", lambda e: e.tensor_scalar(out=lamt[:, 0:1], in0=lamt[:, 0:1], scalar1=float(-lam_init), scalar2=None, op0=ALU.add),
                     reads=["lamt"], writes=["lamt"])

                stage = t("vstage", [128, 128])
                cnt = [0]

                def vecT(src_rows_ap, nrows, dst_ap, func=None):
                    k = cnt[0]; cnt[0] += 1
                    S.dma("sp", dmafn(lambda e: e.dma_start(out=stage[:nrows, :], in_=src_rows_ap), 1), "ld_vs", writes=["vstage"])
                    S.op("pe", lambda e: e.transpose(out=pt[:, :nrows], in_=stage[:nrows, :], identity=ident[:nrows, :nrows]),
                         reads=["vstage", "ident"], writes=["c_pt"])
                    if func is None:
                        S.op("dve", lambda e: e.tensor_copy(out=dst_ap, in_=pt[:, :nrows]), reads=["c_pt"], writes=["vec%d" % k])
                    else:
                        S.op("act", lambda e: e.activation(out=dst_ap, in_=pt[:, :nrows], func=func), reads=["c_pt"], writes=["vec%d" % k])

                for l in range(2):
                    vecT(n_mix[l], KC, nmixT[:, l, :]); vecT(n_mlp[l], KC, nmlpT[:, l, :])
                    for r0 in range(0, 6 * KC, 128):
                        nr = min(128, 6 * KC - r0)
                        vecT(b_mod[l, r0:r0 + nr, :], nr, bmodT[:, l, r0:r0 + nr])
                vecT(fin_g, KC, fingT[:]); vecT(cdw_b, KC, dwbT[:]); vecT(cln_g, KC, clngT[:]); vecT(cln_b, KC, clnbT[:])
                for kc in range(KC):
                    k = cnt[0]; cnt[0] += 1
                    S.dma("sp", dmafn(lambda e, kc=kc: e.dma_start(out=stage[:CW, :], in_=cdw[:, kc * 128:(kc + 1) * 128]), 1), "ld_vs", writes=["vstage"])
                    S.op("pe", lambda e: e.transpose(out=pt[:, :CW], in_=stage[:CW, :], identity=ident[:CW, :CW]), reads=["vstage", "ident"], writes=["c_pt"])
                    S.op("dve", lambda e, kc=kc: e.tensor_copy(out=dwT[:, kc, :], in_=pt[:, :CW]), reads=["c_pt"], writes=["vec%d" % k])
                vecT(c_p, KC, csT[:, :, 0], func=AF.Silu)
                vecT(c_s, KC, csT[:, :, 1], func=AF.Silu)
                S.emit("c0")

        wstate = {"n": 0}

        def load_w(S, src_ap_fn, width=512):
            slot = wstate["n"] % 2
            wstate["n"] += 1
            nsp = 4 if KC >= 4 else 1
            step = KC // nsp

            def fn(e, slot=slot):
                res = []
                for q in range(nsp):
                    res.append(e.dma_start(out=Wb[slot][:, q * step:(q + 1) * step, 0:width], in_=src_ap_fn(q * step, (q + 1) * step)))
                return res
            S.dma("pool", dmafn(fn, nsp), f"ldW{slot}", writes=[f"W{slot}"])
            return slot

        def wsrc(w2d, r0, c0, width):
            v = w2d[r0:r0 + D, c0:c0 + width].rearrange("(k p) n -> p k n", p=128)
            return lambda a, b: v[:, a:b, :]

        def phase_mod():
            S = Sched(nc)
            with ExitStack() as ps:
                alloc_W(ps)
                pm = [ps.enter_context(PST(f"m_ps{i}", [128, 512], F32)) for i in range(2)]
                nblk = 6 * D // 512
                for l in range(2):
                    pend = None
                    slots = {}
                    slots[0] = load_w(S, wsrc(w_mod[l], 0, 0, 512))
                    for j in range(nblk):
                        if j + 1 < nblk:
                            slots[j + 1] = load_w(S, wsrc(w_mod[l], 0, (j + 1) * 512, 512))
                        slot = slots[j]
                        pj = pm[j % 2]

                        def mm(e, slot=slot, pj=pj):
                            ins = None
                            for s in range(4):
                                for kc in range(KC):
                                    ins = e.matmul(pj[:, 2 * s:2 * s + 2], lhsT=Wb[slot][:, kc, s * 128:(s + 1) * 128], rhs=csT[:, kc, :],
                                                   start=(kc == 0), stop=(kc == KC - 1))
                            return ins
                        S.op("pe", mm, reads=[f"W{slot}", "csT"], writes=[f"m_ps{j % 2}"])
                        c0 = j * 4
                        which, kc0 = c0 // KC, c0 % KC

                        def ev(e, l=l, which=which, kc0=kc0, pj=pj, c0=c0):
                            return e.tensor_tensor(out=modT[:, l, which, kc0:kc0 + 4, :],
                                                   in0=pj[:, 0:8].rearrange("p (s k) -> p s k", k=2),
                                                   in1=bmodT[:, l, c0:c0 + 4].unsqueeze(2).broadcast_to([128, 4, 2]), op=ALU.add)
                        S.op("dve", ev, reads=[f"m_ps{j % 2}"], writes=["modT"])
                sqD = float(math.sqrt(D))
                for l in range(2):
                    for mf in range(2):
                        gain = (nmixT if mf == 0 else nmlpT)[:, l, :]
                        for kind in range(2):
                            sh = modT[:, l, 3 * mf + 0, :, kind]; sc = modT[:, l, 3 * mf + 1, :, kind]; gt = modT[:, l, 3 * mf + 2, :, kind]
                            S.op("dve", lambda e, sc=sc, gain=gain, l=l, mf=mf, kind=kind: e.scalar_tensor_tensor(
                                out=PA[:, l, mf, kind, :], in0=sc, scalar=1.0, in1=gain, op0=ALU.add, op1=ALU.mult), reads=["modT"], writes=["PAt"])
                            S.op("dve", lambda e, sh=sh, l=l, mf=mf, kind=kind: e.tensor_copy(out=PB[:, l, mf, kind, :], in_=sh), reads=["modT"], writes=["PBt"])
                            S.op("dve", lambda e, gt=gt, l=l, mf=mf, kind=kind: e.tensor_copy(out=PG[:, l, mf, kind, :], in_=gt), reads=["modT"], writes=["PGt"])
                S.emit("md")

        def phase_xin():
            S = Sched(nc)
            with ExitStack() as ps:
                xin = [ps.enter_context(SBT(f"xin{i}", [128, D], F32)) for i in range(2)]
                xst = [ps.enter_context(SBT(f"xst{i}", [128, KC, 128], F32)) for i in range(1)] * 2
                pp = [ps.enter_context(PST(f"x_ps{i}", [128, 512], F32)) for i in range(2)]
                tl = [("P", i * 128, 128, i * 128) for i in range(NPT)] + [("S", 0, DEC, SEQ)]
                nb = 0
                for ti, (kind, pos0, n, tok0) in enumerate(tl):
                    b = ti % 2
                    src = x_p[pos0:pos0 + n, :] if kind == "P" else x_s[0:n, :]
                    S.dma("sp", dmafn(lambda e, b=b, n=n, src=src: e.dma_start(out=xin[b][:n, :], in_=src), 1), f"ld_xin{b}", writes=[f"xin{b}"])
                    for k0 in range(0, KC, 4):
                        kk = min(4, KC - k0)
                        p = pp[nb % 2]

                        def tr(e, b=b, n=n, k0=k0, kk=kk, p=p):
                            ins = None
                            for q in range(kk):
                                ins = e.transpose(out=p[:, q * 128:q * 128 + n], in_=xin[b][:n, (k0 + q) * 128:(k0 + q + 1) * 128], identity=ident[:n, :n])
                            return ins
                        S.op("pe", tr, reads=[f"xin{b}", "ident"], writes=[f"x_ps{nb % 2}"])
                        eng = "act" if nb % 2 == 0 else "dve"
                        if eng == "act":
                            S.op("act", lambda e, b=b, n=n, k0=k0, kk=kk, p=p: e.activation(
                                out=xst[b][:, k0:k0 + kk, :n], in_=p[:, 0:kk * 128].rearrange("p (k t) -> p k t", t=128)[:, :, :n], func=AF.Copy),
                                reads=[f"x_ps{nb % 2}"], writes=["xst0"])
                        else:
                            S.op("dve", lambda e, b=b, n=n, k0=k0, kk=kk, p=p: e.tensor_copy(
                                out=xst[b][:, k0:k0 + kk, :n], in_=p[:, 0:kk * 128].rearrange("p (k t) -> p k t", t=128)[:, :, :n]),
                                reads=[f"x_ps{nb % 2}"], writes=["xst0"])
                        nb += 1
                    def stx(e, n=n, tok0=tok0):
                        return [e.dma_start(out=xTv[:, k0:k0 + KS, tok0:tok0 + n], in_=xst[0][:, k0:k0 + KS, :n]) for k0 in range(0, KC, KS)]
                    S.dma("sp", dmafn(stx, KC // KS), "st_xst0", reads=["xst0"], writes=[])
                S.emit("xi")

        def phase_norm(group, l, mf, final=False):
            S = Sched(nc)
            with ExitStack() as ps:
                xt = [ps.enter_context(SBT(f"n_xt{i}", [128, KC, 128], F32)) for i in range(2)]
                sq2 = [ps.enter_context(SBT(f"n_sq{i}", [128, KC, 128], F32)) for i in range(2)]
                rstd2 = [ps.enter_context(SBT(f"n_rstd{i}", [128, 128], F32)) for i in range(2)]
                pss2 = [ps.enter_context(PST(f"n_ps{i}", [128, 128], F32)) for i in range(2)]
                if final:
                    ptr = [ps.enter_context(PST(f"n_pt{i}", [128, 512], F32)) for i in range(2)]
                    yst = [ps.enter_context(SBT(f"n_yst{i}", [128, D], F32)) for i in range(2)]
                nb = 0
                for ti, (kind, pos0, n, tok0, acol) in enumerate(tiles_of(group)):
                    b = ti % 2
                    sq = sq2[ti % len(sq2)]
                    sqr = f"n_sq{ti % len(sq2)}"
                    ki = KI[kind]

                    def ldx(e, b=b, n=n, tok0=tok0):
                        return [e.dma_start(out=xt[b][:, k0:k0 + KS, :n], in_=xTv[:, k0:k0 + KS, tok0:tok0 + n]) for k0 in range(0, KC, KS)]
                    S.dma("sp", dmafn(ldx, KC // KS), f"ld_nxt{b}", writes=[f"n_xt{b}"])
                    S.op("act", lambda e, sq=sq, b=b, n=n: e.activation(out=sq[:, :, :n], in_=xt[b][:, :, :n], func=AF.Square), reads=[f"n_xt{b}"], writes=[sqr])
                    rstd = rstd2[b]; pss = pss2[b]; rsr = f"n_rstd{b}"; psr = f"n_ps{b}"

                    def ssum(e, sq=sq, n=n, pss=pss):
                        ins = None
                        for kc in range(KC):
                            ins = e.matmul(pss[:, :n], lhsT=onesf[:, :], rhs=sq[:, kc, :n], start=(kc == 0), stop=(kc == KC - 1))
                        return ins
                    S.op("pe", ssum, reads=[sqr, "onesf"], writes=[psr])
                    S.op("act", lambda e, n=n, rstd=rstd, pss=pss: e.activation(out=rstd[:, :n], in_=pss[:, :n], func=AF.Sqrt, scale=float(1.0 / D), bias=float(NORM_EPS)),
                         reads=[psr], writes=[rsr])
                    S.op("dve", lambda e, n=n, rstd=rstd: e.reciprocal(out=rstd[:, :n], in_=rstd[:, :n]), reads=[rsr], writes=[rsr])
                    S.op("dve", lambda e, sq=sq, b=b, n=n, rstd=rstd: e.tensor_tensor(out=sq[:, :, :n], in0=xt[b][:, :, :n], in1=rstd[:, :n].unsqueeze(1).broadcast_to([128, KC, n]), op=ALU.mult),
                         reads=[f"n_xt{b}", rsr, sqr], writes=[sqr])
                    if not final:
                        A_ap = PA[:, l, mf, ki, :]; B_ap = PB[:, l, mf, ki, :]
                        S.op("dve", lambda e, sq=sq, n=n, A_ap=A_ap: e.tensor_tensor(out=sq[:, :, :n], in0=sq[:, :, :n], in1=A_ap.unsqueeze(2).broadcast_to([128, KC, n]), op=ALU.mult),
                             reads=[sqr, "PAt"], writes=[sqr])
                        S.op("pool", lambda e, n=n, B_ap=B_ap, acol=acol, sq=sq: e.tensor_tensor(out=Abuf[:, :, acol:acol + n], in0=sq[:, :, :n], in1=B_ap.unsqueeze(2).broadcast_to([128, KC, n]), op=ALU.add),
                             reads=[sqr, "PBt"], writes=["Abuf"])
                    else:
                        S.op("dve", lambda e, sq=sq, n=n: e.tensor_tensor(out=sq[:, :, :n], in0=sq[:, :, :n], in1=fingT[:, :].unsqueeze(2).broadcast_to([128, KC, n]), op=ALU.mult),
                             reads=[sqr], writes=[sqr])
                        for k0 in range(0, KC, 4):
                            kk = min(4, KC - k0)
                            p = ptr[nb % 2]

                            def tr(e, n=n, k0=k0, kk=kk, p=p, sq=sq):
                                ins = None
                                for q in range(kk):
                                    ins = e.transpose(out=p[:n, q * 128:(q + 1) * 128], in_=sq[:, k0 + q, :n], identity=ident[:, :])
                                return ins
                            S.op("pe", tr, reads=[sqr, "ident"], writes=[f"n_pt{nb % 2}"])
                            if nb % 2 == 0:
                                S.op("act", lambda e, b=b, n=n, k0=k0, kk=kk, p=p: e.activation(out=yst[b][:n, k0 * 128:(k0 + kk) * 128], in_=p[:n, 0:kk * 128], func=AF.Copy),
                                     reads=[f"n_pt{nb % 2}"], writes=[f"n_yst{b}"])
                            else:
                                S.op("dve", lambda e, b=b, n=n, k0=k0, kk=kk, p=p: e.tensor_copy(out=yst[b][:n, k0 * 128:(k0 + kk) * 128], in_=p[:n, 0:kk * 128]),
                                     reads=[f"n_pt{nb % 2}"], writes=[f"n_yst{b}"])
                            nb += 1
                        dst = y_p[pos0:pos0 + n, :] if kind == "P" else y_s[0:n, :]
                        S.dma("sp", dmafn(lambda e, b=b, n=n, dst=dst: e.dma_start(out=dst, in_=yst[b][:n, :]), 1), f"st_yst{b}", reads=[f"n_yst{b}"], writes=[])
                S.emit("nm")

        def dense_ws(S, group, nblk, wsrc_fn, epilogue, pst, pair=False):
            chunks = chunks_of(group)
            nchunk = len(chunks)
            slots = {0: load_w(S, wsrc_fn(0))}
            cnt = 0
            for j in range(nblk):
                if j + 1 < nblk:
                    slots[j + 1] = load_w(S, wsrc_fn(j + 1))
                slot = slots[j]
                nsub = 2 if pair else 4
                for s in range(nsub):
                    if not pair:
                        banks = [(cnt * nchunk + ci) % len(pst) for ci in range(nchunk)]
                        cnt += 1

                        def mm(e, slot=slot, s=s, banks=banks):
                            ins = None
                            for kc in range(KC):
                                for ci, (kind, pos0, n, tok0, acol) in enumerate(chunks):
                                    ins = e.matmul(pst[banks[ci]][:, :n], lhsT=Wb[slot][:, kc, s * 128:(s + 1) * 128], rhs=Abuf[:, kc, acol:acol + n],
                                                   start=(kc == 0), stop=(kc == KC - 1))
                            return ins
                        S.op("pe", mm, reads=[f"W{slot}", "Abuf"], writes=[f"ps{b_}" for b_ in banks])
                        for ci, ch in enumerate(chunks):
                            epilogue(S, j, s, ci, ch, pst[banks[ci]], f"ps{banks[ci]}")
                    else:
                        for ci, ch in enumerate(chunks):
                            (kind, pos0, n, tok0, acol) = ch
                            ba = (cnt * 2) % len(pst); bb = (cnt * 2 + 1) % len(pst)
                            cnt += 1

                            def mm(e, slot=slot, s=s, ba=ba, bb=bb, n=n, acol=acol):
                                ins = None
                                for kc in range(KC):
                                    ins = e.matmul(pst[ba][:, :n], lhsT=Wb[slot][:, kc, s * 128:(s + 1) * 128], rhs=Abuf[:, kc, acol:acol + n],
                                                   start=(kc == 0), stop=(kc == KC - 1))
                                for kc in range(KC):
                                    ins = e.matmul(pst[bb][:, :n], lhsT=Wb[slot][:, kc, 256 + s * 128:256 + (s + 1) * 128], rhs=Abuf[:, kc, acol:acol + n],
                                                   start=(kc == 0), stop=(kc == KC - 1))
                                return ins
                            S.op("pe", mm, reads=[f"W{slot}", "Abuf"], writes=[f"ps{ba}", f"ps{bb}"])
                            epilogue(S, j, s, ci, ch, (pst[ba], pst[bb]), (f"ps{ba}", f"ps{bb}"))

        def gated_residual_epilogue(stg, l, mf):
            cnt = [0]

            def ep(S, j, s, ci, ch, p, pname):
                (kind, pos0, n, tok0, acol) = ch
                ko = j * 4 + s
                b = cnt[0] % len(stg); cnt[0] += 1
                g_ap = PG[:, l, mf, KI[kind], ko:ko + 1]
                S.op("act", lambda e, b=b, n=n, p=p, g_ap=g_ap: e.activation(out=stg[b][:, :n], in_=p[:, :n], func=AF.Copy, scale=g_ap),
                     reads=[pname, "PGt"], writes=[f"stg{b}"])
                S.dma("pool", dmafn(lambda e, b=b, n=n, ko=ko, tok0=tok0: e.dma_start(out=xTv[:, ko, tok0:tok0 + n], in_=stg[b][:, :n], accum_op=ALU.add), 1),
                      f"st_stg{b}", reads=[f"stg{b}"], writes=[])
            return ep

        def phase_qkv(group):
            S = Sched(nc)
            tl = tiles_of(group)
            nt = len(tl)
            scale = 1.0 / math.sqrt(128.0)
            with ExitStack() as ps:
                alloc_W(ps)
                pq = [ps.enter_context(PST(f"q_ps{i}", [128, 512], F32)) for i in range(4)]
                ptb = [ps.enter_context(PST(f"q_pt{i}", [128, 512], BF16)) for i in range(2)]
                stg = [ps.enter_context(SBT(f"q_stg{i}", [128, 512], F32)) for i in range(3)]
                t1 = ps.enter_context(SBT("q_t1", [128, 512], F32))
                b16 = [ps.enter_context(SBT(f"q_b16{i}", [128, 512], BF16)) for i in range(2)]
                tst = [ps.enter_context(SBT(f"q_tst{i}", [128, 4, 128], BF16)) for i in range(2)]
                rC = ps.enter_context(SBT("q_rC", [128, nt, 128], F32))
                rS = ps.enter_context(SBT("q_rS", [128, nt, 128], F32))
                for ti, (kind, pos0, n, tok0, acol) in enumerate(tl):
                    S.dma("sp", dmafn(lambda e, ti=ti, n=n, tok0=tok0: e.dma_start(out=rC[:n, ti, :], in_=k_ropeC[tok0:tok0 + n, :]), 1), "ld_rC", writes=["q_rC"])
                    S.dma("sp", dmafn(lambda e, ti=ti, n=n, tok0=tok0: e.dma_start(out=rS[:n, ti, :], in_=k_ropeS[tok0:tok0 + n, :]), 1), "ld_rS", writes=["q_rS"])
                nbs = DH // BWQ
                nh = BWQ // 128
                nblk = 6 * nbs
                outs = {("P", 1): kd_p, ("P", 2): vd_p, ("P", 4): ks_p, ("P", 5): vs_p,
                        ("S", 1): kd_s, ("S", 2): vd_s, ("S", 4): ks_s, ("S", 5): vs_s}

                def src_fn(j):
                    sec, jj = j // nbs, j % nbs
                    return wsrc(w_in, 0, sec * DH + jj * BWQ, BWQ)
                slots = {0: load_w(S, src_fn(0), BWQ)}
                cnt = 0
                for j in range(nblk):
                    if j + 1 < nblk:
                        slots[j + 1] = load_w(S, src_fn(j + 1), BWQ)
                    slot = slots[j]
                    sec, jj = j // nbs, j % nbs
                    for ti, (kind, pos0, n, tok0, acol) in enumerate(tl):
                        pb = cnt % 4; sg = cnt % 3; bb = cnt % 2
                        cnt += 1
                        p = pq[pb]

                        def mm(e, slot=slot, n=n, acol=acol, p=p):
                            ins = None
                            for kc in range(KC):
                                ins = e.matmul(p[:n, :BWQ], lhsT=Abuf[:, kc, acol:acol + n], rhs=Wb[slot][:, kc, 0:BWQ], start=(kc == 0), stop=(kc == KC - 1))
                            return ins
                        S.op("pe", mm, reads=[f"W{slot}", "Abuf"], writes=[f"q_ps{pb}"])
                        st = stg[sg]
                        if sec in (0, 1):
                            pv = p[:n, :BWQ].rearrange("t (h x d) -> t h x d", x=2, d=64)
                            sv = st[:n, :BWQ].rearrange("t (h x d) -> t h x d", x=2, d=64)
                            tv = t1[:n, :BWQ].rearrange("t (h x d) -> t h x d", x=2, d=64)
                            cosb = rC[:n, ti, :].rearrange("t (x d) -> t x d", x=2).unsqueeze(1).broadcast_to([n, nh, 2, 64])
                            S.op("dve", lambda e, sv=sv, pv=pv, cosb=cosb: e.tensor_tensor(out=sv, in0=pv, in1=cosb, op=ALU.mult),
                                 reads=[f"q_ps{pb}", "q_rC"], writes=[f"q_stg{sg}"])
                            sin0 = rS[:n, ti, 0:64].unsqueeze(1).broadcast_to([n, nh, 64])
                            sin1 = rS[:n, ti, 64:128].unsqueeze(1).broadcast_to([n, nh, 64])
                            S.op("dve", lambda e, tv=tv, pv=pv, sin0=sin0: e.tensor_tensor(out=tv[:, :, 0, :], in0=pv[:, :, 1, :], in1=sin0, op=ALU.mult),
                                 reads=[f"q_ps{pb}", "q_rS"], writes=["q_t1a"])
                            S.op("dve", lambda e, tv=tv, pv=pv, sin1=sin1: e.tensor_tensor(out=tv[:, :, 1, :], in0=pv[:, :, 0, :], in1=sin1, op=ALU.mult),
                                 reads=[f"q_ps{pb}", "q_rS"], writes=["q_t1b"])
                            S.op("dve", lambda e, st=st, n=n: e.tensor_tensor(out=st[:n, :BWQ], in0=st[:n, :BWQ], in1=t1[:n, :BWQ], op=ALU.add),
                                 reads=[f"q_stg{sg}", "q_t1a", "q_t1b"], writes=[f"q_stg{sg}"])
                        else:
                            S.op("act", lambda e, st=st, n=n, p=p: e.activation(out=st[:n, :BWQ], in_=p[:n, :BWQ], func=AF.Copy),
                                 reads=[f"q_ps{pb}"], writes=[f"q_stg{sg}"])
                        if sec in (1, 2, 4, 5):
                            dst = outs[(kind, sec)][pos0:pos0 + n, jj * BWQ:(jj + 1) * BWQ]
                            S.dma("sp", dmafn(lambda e, st=st, n=n, dst=dst: e.dma_start(out=dst, in_=st[:n, :BWQ]), 1), f"st_qstg{sg}",
                                  reads=[f"q_stg{sg}"], writes=[])
                        bt = b16[bb]
                        if sec in (0, 3):
                            S.op("act", lambda e, bt=bt, st=st, n=n: e.activation(out=bt[:n, :BWQ], in_=st[:n, :BWQ], func=AF.Copy, scale=float(scale)),
                                 reads=[f"q_stg{sg}"], writes=[f"q_b16{bb}"])
                        else:
                            S.op("act", lambda e, bt=bt, st=st, n=n: e.activation(out=bt[:n, :BWQ], in_=st[:n, :BWQ], func=AF.Copy),
                                 reads=[f"q_stg{sg}"], writes=[f"q_b16{bb}"])
                        if sec in (2, 5):
                            c0 = (0 if sec == 2 else DH) + jj * BWQ
                            S.dma("sp", dmafn(lambda e, bt=bt, n=n, tok0=tok0, c0=c0: e.dma_start(out=vbf[tok0:tok0 + n, c0:c0 + BWQ], in_=bt[:n, :BWQ]), 1),
                                  f"st_qb16{bb}", reads=[f"q_b16{bb}"], writes=[])
                        else:
                            pt_ = ptb[bb]

                            def tr(e, bt=bt, n=n, pt_=pt_):
                                ins = None
                                for h in range(nh):
                                    ins = e.transpose(out=pt_[:, h * 128:h * 128 + n], in_=bt[:n, h * 128:(h + 1) * 128], identity=identb[:n, :n])
                                return ins
                            S.op("pe", tr, reads=[f"q_b16{bb}", "identb"], writes=[f"q_pt{bb}"])
                            ts_ = tst[bb]
                            S.op("dve", lambda e, ts_=ts_, pt_=pt_, n=n: e.tensor_copy(out=ts_[:, 0:nh, :n], in_=pt_[:, 0:nh * 128].rearrange("p (h t) -> p h t", t=128)[:, :, :n]),
                                 reads=[f"q_pt{bb}"], writes=[f"q_tst{bb}"])
                            isq = sec in (0, 3)
                            hbase = (0 if sec in (0, 1) else NQA) + jj * nh
                            dstT = (qT if isq else kT)[hbase:hbase + nh, :, tok0:tok0 + n].rearrange("h p t -> p h t")
                            S.dma("sp", dmafn(lambda e, ts_=ts_, n=n, dstT=dstT: e.dma_start(out=dstT, in_=ts_[:, 0:nh, :n]), 1),
                                  f"st_qtst{bb}", reads=[f"q_tst{bb}"], writes=[])
                S.emit("qk")

        def phase_attn(group):
            S = Sched(nc)
            tl = tiles_of(group)
            pt_tiles = [t for t in tl if t[0] == "P"]
            s_tiles = [t for t in tl if t[0] == "S"]
            nkp = (max(t[1] for t in pt_tiles) + 128) if pt_tiles else 0
            NKMAX = max(nkp, (PAST + DEC) if s_tiles else 0)
            NKT = (NKMAX + 127) // 128
            NS = 2
            with ExitStack() as ps:
                def t_(name, shape, dt=F32):
                    return ps.enter_context(SBT(name, list(shape), dt))
                pS = [[ps.enter_context(PST(f"a_pS{s}_{i}", [128, 512], F32)) for i in range(2)] for s in range(NS)]
                pT = [ps.enter_context(PST(f"a_pT{s}", [128, 1024], BF16)) for s in range(NS)]
                pOO = [ps.enter_context(PST(f"a_pOO{s}", [128, 512], F32)) for s in range(NS)]
                pO = [pOO[s][:, 0:256] for s in range(NS)]
                pOT = [pOO[s][:, 256:512].bitcast(BF16) for s in range(NS)]
                KTp = [t_(f"a_KTp{i}", [128, max(nkp, 128)], BF16) for i in range(2)]
                QTb = [t_(f"a_QT{i}", [128, GMAX], BF16) for i in range(2)]
                Vp = [t_("a_Vp0", [128, NKT, 256], BF16)]
                if s_tiles:
                    KTs = [t_(f"a_KTs{i}", [128, PAST + DEC], BF16) for i in range(2)]
                    Kc = [t_(f"a_Kc{i}", [128, PAST // 128, 128], BF16) for i in range(2)]
                    Vs = [t_("a_Vs0", [128, PAST // 128 + 1, 256], BF16)]
                E = [[t_(f"a_E{s}_{i}", [128, NKMAX], F32) for i in range(2)] for s in range(NS)]
                Wt = [t_(f"a_W{s}", [128, NKMAX], BF16) for s in range(NS)]
                WT = [t_(f"a_WT{s}", [128, NKT, 128], BF16) for s in range(NS)]
                dg2 = [[t_(f"a_dg{s}_{c}", [128, 128]) for c in range(2)] for s in range(NS)]
                mx = [t_(f"a_mx{s}", [128, 2, 8]) for s in range(NS)]
                sm = [t_(f"a_sm{s}", [128, 2, 8]) for s in range(NS)]
                sc = [t_(f"a_sc{s}", [128, 8]) for s in range(NS)]
                eb = [[t_(f"a_eb{s}_{i}", [128, 512], F32) for i in range(2)] for s in range(NS)]
                ob = [t_(f"a_ob{s}", [128, 256], BF16) for s in range(NS)]
                onec = t_("a_one", [128, 1])
                S.op("dve", lambda e: e.memset(onec[:], 1.0), writes=["a_one"])
                ccnt = [0]

                def attn_tile(st, kindA, QT, KT, Vt, ve, tile, nk_full_tiles, dn, kcol_diag, kc_out):
                    (kind, pos0, n, tok0, acol) = tile
                    nk = nk_full_tiles * 128 + dn
                    nkt = nk_full_tiles + 1
                    chunks = [(c0, min(512, nk_full_tiles * 128 - c0)) for c0 in range(0, nk_full_tiles * 128, 512)]
                    ncomp = len(QT)
                    R = lambda nm: f"{nm}{st}"
                    E0, E1 = E[st]
                    W_, WT_, mx_, sm_, sc_, ob_t = Wt[st], WT[st], mx[st], sm[st], sc[st], ob[st]
                    pcnt = [0]

                    def nextps():
                        i = pcnt[0] % 2; pcnt[0] += 1
                        return pS[st][i], f"a_pS{st}_{i}"
                    nch = len(chunks) + 1
                    if kindA == "diff":
                        ek = [[f"a_E{st}_{c}_{i}" for i in range(nch)] for c in range(ncomp)]
                        for c in range(ncomp):
                            qap, qres = QT[c]; kap, kres = KT[c]
                            Ec = E[st][c]
                            dgc = dg2[st][c]; dgr = f"a_dg{st}_{c}"
                            mk = [f"a_mx{st}_{c}_{i}" for i in range(nch)]
                            sk = [f"a_sm{st}_{c}_{i}" for i in range(nch)]
                            scm = f"a_scm{st}_{c}"; scr = f"a_scr{st}_{c}"
                            p, pr = nextps()
                            S.op("pe", lambda e, p=p, qap=qap, kap=kap: e.matmul(p[:n, :dn], lhsT=qap, rhs=kap[:, kcol_diag:kcol_diag + dn], start=True, stop=True),
                                 reads=[qres, kres], writes=[pr]); yield
                            S.op("dve", lambda e, p=p, dgc=dgc: e.tensor_tensor(out=dgc[:n, :dn], in0=p[:n, :dn], in1=mdiff[:n, :dn], op=ALU.add),
                                 reads=[pr, "mdiff"], writes=[dgr]); yield
                            S.op("dve", lambda e, c=c, dgc=dgc: e.tensor_reduce(out=mx_[:n, c, 0:1], in_=dgc[:n, :dn], axis=AX.X, op=ALU.max), reads=[dgr], writes=[mk[0]]); yield
                            for qi, (c0, w) in enumerate(chunks):
                                p, pr = nextps()
                                S.op("pe", lambda e, p=p, qap=qap, kap=kap, c0=c0, w=w: e.matmul(p[:n, :w], lhsT=qap, rhs=kap[:, c0:c0 + w], start=True, stop=True),
                                     reads=[qres, kres], writes=[pr]); yield
                                S.op("dve", lambda e, p=p, c=c, qi=qi, w=w: e.tensor_reduce(out=mx_[:n, c, qi + 1:qi + 2], in_=p[:n, :w], axis=AX.X, op=ALU.max),
                                     reads=[pr], writes=[mk[qi + 1]]); yield
                            S.op("dve", lambda e, c=c: e.tensor_reduce(out=sc_[:n, c:c + 1], in_=mx_[:n, c, 0:nch], axis=AX.X, op=ALU.max), reads=mk, writes=[scm]); yield
                            S.op("dve", lambda e, c=c: e.tensor_scalar(out=sc_[:n, c:c + 1], in0=sc_[:n, c:c + 1], scalar1=-1.0, scalar2=None, op0=ALU.mult), reads=[scm], writes=[scm]); yield
                            S.op("act", lambda e, c=c, Ec=Ec, dgc=dgc: e.activation(out=Ec[:n, nk_full_tiles * 128:nk], in_=dgc[:n, :dn], func=AF.Exp, bias=sc_[:n, c:c + 1], accum_out=sm_[:n, c, 0:1]),
                                 reads=[dgr, scm], writes=[ek[c][0], sk[0]]); yield
                            for qi, (c0, w) in enumerate(chunks):
                                p, pr = nextps()
                                S.op("pe", lambda e, p=p, qap=qap, kap=kap, c0=c0, w=w: e.matmul(p[:n, :w], lhsT=qap, rhs=kap[:, c0:c0 + w], start=True, stop=True),
                                     reads=[qres, kres], writes=[pr]); yield
                                S.op("act", lambda e, p=p, c=c, qi=qi, c0=c0, w=w, Ec=Ec: e.activation(out=Ec[:n, c0:c0 + w], in_=p[:n, :w], func=AF.Exp, bias=sc_[:n, c:c + 1],
                                                                                                 accum_out=sm_[:n, c, qi + 1:qi + 2]),
                                     reads=[pr, scm], writes=[ek[c][qi + 1], sk[qi + 1]]); yield
                            S.op("dve", lambda e, c=c: e.tensor_reduce(out=sc_[:n, 2 + c:3 + c], in_=sm_[:n, c, 0:nch], axis=AX.X, op=ALU.add), reads=sk, writes=[scr]); yield
                            S.op("dve", lambda e, c=c: e.reciprocal(out=sc_[:n, 2 + c:3 + c], in_=sc_[:n, 2 + c:3 + c]), reads=[scr], writes=[scr]); yield
                        scr0, scr1 = f"a_scr{st}_0", f"a_scr{st}_1"
                        S.op("dve", lambda e: e.tensor_tensor(out=sc_[:n, 3:4], in0=sc_[:n, 3:4], in1=lamt[:n, 0:1], op=ALU.mult), reads=[scr1, "lamt"], writes=[scr1]); yield
                        S.op("dve", lambda e: e.tensor_scalar(out=E1[:n, :nk], in0=E1[:n, :nk], scalar1=sc_[:n, 3:4], scalar2=None, op0=ALU.mult),
                             reads=ek[1] + [scr1], writes=ek[1]); yield
                        S.op("dve", lambda e: e.scalar_tensor_tensor(out=W_[:n, :nk], in0=E0[:n, :nk], scalar=sc_[:n, 2:3], in1=E1[:n, :nk], op0=ALU.mult, op1=ALU.add),
                             reads=ek[0] + ek[1] + [scr0], writes=[R("a_W")]); yield
                    else:
                        qap, qres = QT[0]; kap, kres = KT[0]
                        Lb, NB = E0, E1
                        lk = [f"a_E{st}_0_{i}" for i in range(nch)]
                        nkk = [f"a_E{st}_1_{i}" for i in range(nch)]
                        d0 = nk_full_tiles * 128
                        order = [(-1, d0, dn)] + [(qi, c0, w) for qi, (c0, w) in reversed(list(enumerate(chunks)))]
                        prev_key = None
                        for oi, (qi, c0, w) in enumerate(order):
                            isd = (qi == -1)
                            ki_ = 0 if isd else qi + 1
                            kc0 = kcol_diag if isd else c0
                            p, pr = nextps()
                            ebb = oi % 2
                            ebt = eb[st][ebb]; ebr = f"a_eb{st}_{ebb}"
                            S.op("pe", lambda e, p=p, kc0=kc0, w=w: e.matmul(p[:n, :w], lhsT=qap, rhs=kap[:, kc0:kc0 + w], start=True, stop=True),
                                 reads=[qres, kres], writes=[pr]); yield
                            S.op("act", lambda e, p=p, ebt=ebt, w=w: e.activation(out=ebt[:n, :w], in_=p[:n, :w], func=AF.Exp, scale=-1.0),
                                 reads=[pr], writes=[ebr]); yield
                            S.op("act", lambda e, ebt=ebt, c0=c0, w=w: e.activation(out=Lb[:n, c0:c0 + w], in_=ebt[:n, :w], func=AF.Ln, bias=1.0),
                                 reads=[ebr], writes=[lk[ki_]]); yield
                            if isd:
                                S.op("dve", lambda e, p=p, c0=c0, w=w: e.tensor_tensor(out=NB[:n, c0:c0 + w], in0=p[:n, :w], in1=Lb[:n, c0:c0 + w], op=ALU.add),
                                     reads=[pr, lk[ki_]], writes=[nkk[ki_]]); yield
                                S.op("dve", lambda e, c0=c0, w=w: e.tensor_tensor(out=NB[:n, c0:c0 + w], in0=NB[:n, c0:c0 + w], in1=msb01[:n, :w], op=ALU.mult),
                                     reads=[nkk[ki_], "msb01"], writes=[nkk[ki_]]); yield
                                S.op("dve", lambda e, c0=c0, w=w: e.tensor_tensor_scan(out=NB[:n, c0:c0 + w][:, ::-1], data0=onec[:n, 0:1].broadcast_to([n, w]),
                                                                                     data1=NB[:n, c0:c0 + w][:, ::-1], initial=0.0, op0=ALU.mult, op1=ALU.add),
                                     reads=[nkk[ki_], "a_one"], writes=[nkk[ki_]]); yield
                            else:
                                S.op("dve", lambda e, p=p, c0=c0, w=w: e.tensor_tensor_scan(out=NB[:n, c0:c0 + w][:, ::-1], data0=p[:n, :w][:, ::-1],
                                                                                          data1=Lb[:n, c0:c0 + w][:, ::-1], initial=NB[:n, c0 + w:c0 + w + 1],
                                                                                          op0=ALU.add, op1=ALU.add),
                                     reads=[pr, lk[ki_], prev_key], writes=[nkk[ki_]]); yield
                            prev_key = nkk[ki_]
                        S.op("dve", lambda e: e.tensor_tensor(out=Lb[:n, 0:nk - 1], in0=NB[:n, 1:nk], in1=Lb[:n, 0:nk - 1], op=ALU.add),
                             reads=nkk + lk, writes=lk); yield
                        S.op("dve", lambda e: e.memset(Lb[:n, nk - 1:nk], 0.0), reads=lk, writes=lk); yield
                        S.op("dve", lambda e: e.tensor_tensor(out=Lb[:n, nk - dn:nk], in0=Lb[:n, nk - dn:nk], in1=msbneg[:n, :dn], op=ALU.subtract),
                             reads=lk + ["msbneg"], writes=lk); yield
                        S.op("act", lambda e: e.activation(out=W_[:n, :nk], in_=Lb[:n, :nk], func=AF.Exp, scale=-1.0), reads=lk, writes=[R("a_W")]); yield
                    for k0 in range(0, nkt, 8):
                        kk = min(8, nkt - k0)

                        def tr(e, k0=k0, kk=kk):
                            ins = None
                            for q in range(kk):
                                kt = k0 + q
                                kw = dn if kt == nkt - 1 else 128
                                ins = e.transpose(out=pT[st][:kw, q * 128:q * 128 + n], in_=W_[:n, kt * 128:kt * 128 + kw], identity=identb[:n, :n])
                            return ins
                        S.op("pe", tr, reads=[R("a_W"), "identb"], writes=[R("a_pT")]); yield
                        full = kk if (k0 + kk < nkt) else kk - 1
                        if full > 0:
                            if (k0 // 8) % 2 == 0:
                                S.op("act", lambda e, k0=k0, full=full: e.activation(out=WT_[:, k0:k0 + full, :n], in_=pT[st][:, 0:full * 128].rearrange("p (k t) -> p k t", t=128)[:, :, :n], func=AF.Copy),
                                     reads=[R("a_pT")], writes=[R("a_WT")]); yield
                            else:
                                S.op("dve", lambda e, k0=k0, full=full: e.tensor_copy(out=WT_[:, k0:k0 + full, :n], in_=pT[st][:, 0:full * 128].rearrange("p (k t) -> p k t", t=128)[:, :, :n]),
                                     reads=[R("a_pT")], writes=[R("a_WT")]); yield
                        if full < kk:
                            S.op("act", lambda e, kk=kk: e.activation(out=WT_[:dn, nkt - 1, :n], in_=pT[st][:dn, (kk - 1) * 128:(kk - 1) * 128 + n], func=AF.Copy),
                                 reads=[R("a_pT")], writes=[R("a_WT")]); yield
                    vap, vres = Vt

                    def pv(e):
                        ins = None
                        for kt in range(nkt):
                            kw = dn if kt == nkt - 1 else 128
                            ins = e.matmul(pO[st][:n, :ve], lhsT=WT_[:kw, kt, :n], rhs=vap(kt, kw), start=(kt == 0), stop=(kt == nkt - 1))
                        return ins
                    S.op("pe", pv, reads=[R("a_WT"), vres], writes=[R("a_pO")]); yield
                    if kindA == "diff":
                        S.op("act", lambda e: e.activation(out=eb[st][0][:n, :ve], in_=pO[st][:n, :ve], func=AF.Square, accum_out=sc_[:n, 4:5]),
                             reads=[R("a_pO")], writes=[f"a_eb{st}_0", R("a_sc4")]); yield
                        S.op("act", lambda e: e.activation(out=sc_[:n, 4:5], in_=sc_[:n, 4:5], func=AF.Ln, scale=float(1.0 / ve), bias=float(SUBLN_EPS)), reads=[R("a_sc4")], writes=[R("a_sc4")]); yield
                        S.op("act", lambda e: e.activation(out=sc_[:n, 4:5], in_=sc_[:n, 4:5], func=AF.Exp, scale=-0.5), reads=[R("a_sc4")], writes=[R("a_sc4")]); yield
                        S.op("dve", lambda e: e.scalar_tensor_tensor(out=ob_t[:n, :ve], in0=pO[st][:n, :ve], scalar=sc_[:n, 4:5], in1=sgbc[:n, :ve], op0=ALU.mult, op1=ALU.mult),
                             reads=[R("a_pO"), R("a_sc4"), "sgbc"], writes=[R("a_ob")]); yield
                    else:
                        S.op("act", lambda e: e.activation(out=ob_t[:n, :ve], in_=pO[st][:n, :ve], func=AF.Copy), reads=[R("a_pO")], writes=[R("a_ob")]); yield
                    ne = ve // 128

                    def tro(e):
                        ins = None
                        for q in range(ne):
                            ins = e.transpose(out=pOT[st][:, q * 128:q * 128 + n], in_=ob_t[:n, q * 128:(q + 1) * 128], identity=identb[:n, :n])
                        return ins
                    S.op("pe", tro, reads=[R("a_ob"), "identb"], writes=[R("a_pOT")]); yield
                    S.op("dve", lambda e: e.tensor_copy(out=Abuf[:, kc_out:kc_out + ne, acol:acol + n], in_=pOT[st][:, 0:ne * 128].rearrange("p (k t) -> p k t", t=128)[:, :, :n]),
                         reads=[R("a_pOT")], writes=[f"Abuf_o{st}"]); yield

                def run_items(items):
                    pending = list(items)
                    active = [None] * NS
                    if len(pending) >= 2:
                        S.dry = True
                        nops = sum(1 for _ in pending[0](0))
                        S.dry = False
                        active[0] = pending.pop(0)(0)
                        for _ in range(nops // 2):
                            next(active[0])
                    while True:
                        for s_ in range(NS):
                            if active[s_] is None and pending:
                                active[s_] = pending.pop(0)(s_)
                        if all(a is None for a in active):
                            break
                        for s_ in range(NS):
                            if active[s_] is not None:
                                try:
                                    next(active[s_])
                                except StopIteration:
                                    active[s_] = None

                def run_head(kindA, h):
                    if kindA == "diff":
                        qh = [2 * h, 2 * h + 1]; kh = qh
                        ve = 256; vcol = h * 256; kc_out = 2 * h
                        ck, cvv = ck_d, cv_d
                    else:
                        qh = [NQA + h]; kh = qh
                        ve = 128; vcol = DH + h * 128; kc_out = KC // 2 + h
                        ck, cvv = ck_s, cv_s
                    ncomp = len(qh)
                    for c in range(ncomp):
                        def ldq(e, c=c):
                            res = []
                            for (kind, pos0, nn, tok0, acol0) in chunks_of(group):
                                res.append(e.dma_start(out=QTb[c][:, acol0:acol0 + nn], in_=qT[qh[c], :, tok0:tok0 + nn]))
                            return res
                        S.dma("sp", dmafn(ldq, len(group)), f"ld_QT{c}", writes=[f"a_QT{c}"])
                        if pt_tiles:
                            S.dma("sp", dmafn(lambda e, c=c: e.dma_start(out=KTp[c][:, 0:nkp], in_=kT[kh[c], :, 0:nkp]), 1), f"ld_KTp{c}", writes=[f"a_KTp{c}"])
                    items = []
                    if pt_tiles:
                        def ldvp(e):
                            vv = vbf[0:nkp, vcol:vcol + ve].rearrange("(k p) e -> p k e", p=128)
                            return [e.dma_start(out=Vp[0][:, k0:min(k0 + 8, nkp // 128), 0:ve], in_=vv[:, k0:min(k0 + 8, nkp // 128), :]) for k0 in range(0, nkp // 128, 8)]
                        S.dma("sp", dmafn(ldvp, (nkp // 128 + 7) // 8), "ld_Vp", writes=["a_Vp0"])
                        for tile in pt_tiles:
                            (kind, pos0, n, tok0, acol) = tile
                            nfull = pos0 // 128
                            items.append(lambda st, tile=tile, n=n, acol=acol, nfull=nfull, pos0=pos0: attn_tile(
                                st, kindA, [(QTb[c][:, acol:acol + n], f"a_QT{c}") for c in range(ncomp)],
                                [(KTp[c], f"a_KTp{c}") for c in range(ncomp)],
                                (lambda kt, kw: Vp[0][:kw, kt, 0:ve], "a_Vp0"), ve, tile, nfull, 128, pos0, kc_out))
                    if s_tiles:
                        tile = s_tiles[0]
                        (kind, pos0, n, tok0, acol) = tile
                        npast = PAST // 128
                        for c in range(ncomp):
                            kcol = (kh[c] if kindA == "diff" else h) * 128
                            S.dma("pool", dmafn(lambda e, c=c, kcol=kcol: e.dma_start(out=Kc[c][:, :, :], in_=ck[:, kcol:kcol + 128].rearrange("(k p) d -> p k d", p=128)), 1),
                                  f"ld_Kc{c}", writes=[f"a_Kc{c}"])
                            for k0 in range(0, npast, 8):
                                kk = min(8, npast - k0)
                                tb = ccnt[0] % NS; ccnt[0] += 1

                                def trk(e, c=c, k0=k0, kk=kk, tb=tb):
                                    ins = None
                                    for q in range(kk):
                                        ins = e.transpose(out=pT[tb][:, q * 128:(q + 1) * 128], in_=Kc[c][:, k0 + q, :], identity=identb[:, :])
                                    return ins
                                S.op("pe", trk, reads=[f"a_Kc{c}", "identb"], writes=[f"a_pT{tb}"])
                                S.op("act", lambda e, c=c, k0=k0, kk=kk, tb=tb: e.activation(out=KTs[c][:, k0 * 128:(k0 + kk) * 128], in_=pT[tb][:, 0:kk * 128], func=AF.Copy),
                                     reads=[f"a_pT{tb}"], writes=[f"a_KTs{c}"])
                            S.dma("sp", dmafn(lambda e, c=c: e.dma_start(out=KTs[c][:, PAST:PAST + n], in_=kT[kh[c], :, tok0:tok0 + n]), 1), f"ld_KTs{c}",
                                  writes=[f"a_KTs{c}"])
                        vc0 = vcol if kindA == "diff" else h * 128
                        S.dma("pool", dmafn(lambda e, vc0=vc0: e.dma_start(out=Vs[0][:, 0:npast, 0:ve], in_=cvv[:, vc0:vc0 + ve].rearrange("(k p) e -> p k e", p=128)), 1),
                              "ld_Vs", writes=["a_Vs0"])
                        S.dma("sp", dmafn(lambda e: e.dma_start(out=Vs[0][:n, npast, 0:ve], in_=vbf[tok0:tok0 + n, vcol:vcol + ve]), 1), "ld_Vs2", writes=["a_Vs0"])
                        items.append(lambda st: attn_tile(
                            st, kindA, [(QTb[c][:, acol:acol + n], f"a_QT{c}") for c in range(ncomp)],
                            [(KTs[c], f"a_KTs{c}") for c in range(ncomp)],
                            (lambda kt, kw: Vs[0][:kw, kt, 0:ve], "a_Vs0"), ve, tile, npast, n, PAST, kc_out))
                    run_items(items[::-1])

                for h in range(HA):
                    run_head("diff", h)
                for h in range(HB):
                    run_head("sb", h)
                S.emit("at")

        def phase_proj_resid(group, wmat, l, name):
            S = Sched(nc)
            with ExitStack() as ps:
                alloc_W(ps); pst = [ps.enter_context(PST(f"ps{i}", [128, 512], F32)) for i in range(6)]
                stg = [ps.enter_context(SBT(f"stg{i}", [128, 512], F32)) for i in range(3)]
                dense_ws(S, group, D // 512, lambda j: wsrc(wmat, 0, j * 512, 512), gated_residual_epilogue(stg, l, 0), pst)
                S.emit(name)

        def gcols(group):
            return [(c[3], c[2], c[4]) for c in chunks_of(group)]

        def phase_mlp(group, l):
            S = Sched(nc)
            nch_ = len(chunks_of(group))
            with ExitStack() as ps:
                alloc_W(ps); pst = [ps.enter_context(PST(f"ps{i}", [128, 512], F32)) for i in range(6)]
                rl = [ps.enter_context(SBT(f"u_rl{i}", [128, 512], F32)) for i in range(2)]
                hb = [ps.enter_context(SBT(f"u_hb{i}", [128, 512], BF16)) for i in range(3)]
                stg = [ps.enter_context(SBT(f"stg{i}", [128, 512], F32)) for i in range(3)]
                cnt = [0]

                def ep(S_, j, s, ci, ch, p, pname):
                    (kind, pos0, n, tok0, acol) = ch
                    ko = j * 4 + s
                    r = cnt[0] % 2; b = cnt[0] % 3; cnt[0] += 1
                    S_.op("act", lambda e, r=r, n=n, p=p: e.activation(out=rl[r][:, :n], in_=p[:, :n], func=AF.Relu), reads=[pname], writes=[f"u_rl{r}"])
                    S_.op("dve", lambda e, r=r, b=b, n=n: e.tensor_tensor(out=hb[b][:, :n], in0=rl[r][:, :n], in1=rl[r][:, :n], op=ALU.mult), reads=[f"u_rl{r}"], writes=[f"u_hb{b}"])
                    S_.dma("sp", dmafn(lambda e, b=b, n=n, ko=ko, tok0=tok0: e.dma_start(out=gTv[:, ko, tok0:tok0 + n], in_=hb[b][:, :n]), 1), f"st_uhb{b}",
                           reads=[f"u_hb{b}"], writes=[f"gT_{ko}_{ci}"])
                dense_ws(S, group, DFF // 512, lambda j: wsrc(m_up[l], 0, j * 512, 512), ep, pst)
                for s4 in range(DFF // D):
                    def lda(e, s4=s4):
                        res = []
                        for (tok0, n, acol) in gcols(group):
                            for k0 in range(0, KC, KS):
                                res.append(e.dma_start(out=Abuf[:, k0:k0 + KS, acol:acol + n], in_=gTv[:, s4 * KC + k0:s4 * KC + k0 + KS, tok0:tok0 + n]))
                        return res
                    S.dma("sp", dmafn(lda, len(group) * (KC // KS)), "ld_A",
                          reads=[f"gT_{ko}_{ci}" for ko in range(s4 * KC, (s4 + 1) * KC) for ci in range(nch_)], writes=["Abuf"])
                    dense_ws(S, group, D // 512, lambda j, s4=s4: wsrc(m_dn[l], s4 * D, j * 512, 512), gated_residual_epilogue(stg, l, 1), pst)
                S.emit("ml")

        def phase_conv(group, last_group):
            S = Sched(nc)
            with ExitStack() as ps:
                alloc_W(ps); pst = [ps.enter_context(PST(f"ps{i}", [128, 512], F32)) for i in range(6)]
                sgm = [ps.enter_context(SBT(f"g_sg{i}", [128, 512], F32)) for i in range(2)]
                ust = [ps.enter_context(SBT(f"g_u{i}", [128, 512], F32)) for i in range(3)]
                cnt = [0]

                def ep(S_, j, s, ci, ch, p, pname):
                    (kind, pos0, n, tok0, acol) = ch
                    ko = j * 2 + s
                    r = cnt[0] % 2; b = cnt[0] % 3; cnt[0] += 1
                    S_.op("act", lambda e, r=r, n=n, p=p: e.activation(out=sgm[r][:, :n], in_=p[1][:, :n], func=AF.Sigmoid), reads=[pname[1]], writes=[f"g_sg{r}"])
                    S_.op("dve", lambda e, r=r, b=b, n=n, p=p: e.tensor_tensor(out=ust[b][:, :n], in0=p[0][:, :n], in1=sgm[r][:, :n], op=ALU.mult),
                          reads=[pname[0], f"g_sg{r}"], writes=[f"g_u{b}"])
                    dst = (uTpv[:, ko, CW - 1 + pos0:CW - 1 + pos0 + n] if kind == "P" else uTsv[:, ko, CW - 1:CW - 1 + n])
                    S_.dma("sp", dmafn(lambda e, b=b, n=n, dst=dst: e.dma_start(out=dst, in_=ust[b][:, :n]), 1), f"st_gu{b}", reads=[f"g_u{b}"], writes=[])

                def src(j):
                    va = pw1[:, j * 256:(j + 1) * 256].rearrange("(k p) n -> p k n", p=128)
                    vb_ = pw1[:, D + j * 256:D + (j + 1) * 256].rearrange("(k p) n -> p k n", p=128)
                    return (va, vb_)

                def load_pair(j):
                    slot = wstate["n"] % 2
                    wstate["n"] += 1
                    va, vb_ = src(j)
                    hs = max(1, KC // 4)

                    def fn(e, slot=slot):
                        res = []
                        for (v, c0) in ((va, 0), (vb_, 256)):
                            for q in range(0, KC, hs):
                                res.append(e.dma_start(out=Wb[slot][:, q:q + hs, c0:c0 + 256], in_=v[:, q:q + hs, :]))
                        return res
                    S.dma("pool", dmafn(fn, 2 * (KC // hs)), f"ldW{slot}", writes=[f"W{slot}"])
                    return slot
                chunks = chunks_of(group)
                nblk = D // 256
                slots = {0: load_pair(0)}
                c2 = 0
                for j in range(nblk):
                    if j + 1 < nblk:
                        slots[j + 1] = load_pair(j + 1)
                    slot = slots[j]
                    for s in range(2):
                        for ci, ch in enumerate(chunks):
                            (kind, pos0, n, tok0, acol) = ch
                            ba = (c2 * 2) % 6; bb = (c2 * 2 + 1) % 6
                            c2 += 1

                            def mm(e, slot=slot, s=s, ba=ba, bb=bb, n=n, acol=acol):
                                ins = None
                                for kc in range(KC):
                                    ins = e.matmul(pst[ba][:, :n], lhsT=Wb[slot][:, kc, s * 128:(s + 1) * 128], rhs=Abuf[:, kc, acol:acol + n], start=(kc == 0), stop=(kc == KC - 1))
                                for kc in range(KC):
                                    ins = e.matmul(pst[bb][:, :n], lhsT=Wb[slot][:, kc, 256 + s * 128:256 + (s + 1) * 128], rhs=Abuf[:, kc, acol:acol + n], start=(kc == 0), stop=(kc == KC - 1))
                                return ins
                            S.op("pe", mm, reads=[f"W{slot}", "Abuf"], writes=[f"ps{ba}", f"ps{bb}"])
                            ep(S, j, s, ci, ch, (pst[ba], pst[bb]), (f"ps{ba}", f"ps{bb}"))
                S.emit("p1")
            S = Sched(nc)
            chunks = chunks_of(group)
            ntok = sum(c[2] for c in chunks)
            segs = []
            for (kind, pos0, n, tok0, acol) in chunks:
                if segs and segs[-1][0] == kind and segs[-1][1] + segs[-1][2] == pos0:
                    segs[-1][2] += n
                else:
                    segs.append([kind, pos0, n])
            LU = sum(sg[2] + CW - 1 for sg in segs)
            LY = LU - (CW - 1)
            ycol = []
            for (kind, pos0, n, tok0, acol) in chunks:
                o = 0
                for sg in segs:
                    if sg[0] == kind and sg[1] <= pos0 < sg[1] + sg[2]:
                        ycol.append(o + pos0 - sg[1])
                        break
                    o += sg[2] + CW - 1
            NW = 4
            engs = ["dve", "dve", "dve", "dve"]
            with ExitStack() as ps:
                uin = [ps.enter_context(SBT(f"d_u{i}", [128, LU], F32)) for i in range(NW)]
                uin2 = [ps.enter_context(SBT(f"d_v{i}", [128, LU], F32)) for i in range(NW)]
                for i in range(NW):
                    S.op("dve", lambda e, i=i: e.memset(uin2[i][:], 0.0), writes=[f"d_v{i}"])
                yb = [ps.enter_context(SBT(f"d_y{i}", [128, LU], F32)) for i in range(NW)]
                ysq = [ps.enter_context(SBT(f"d_q{i}", [128, LU], F32)) for i in range(NW)]
                pst = [ps.enter_context(PST(f"d_ps{i}", [128, 512], F32)) for i in range(2 * len(chunks))]
                mean = ps.enter_context(SBT("d_mean", [128, ntok], F32))
                rstd = ps.enter_context(SBT("d_rstd", [128, ntok], F32))
                for k0 in range(0, KC, NW):
                    wave = list(range(k0, min(KC, k0 + NW)))
                    for b, kc in enumerate(wave):
                        def ldu(e, b=b, kc=kc):
                            res = []
                            o = 0
                            for (kind, pos0, n) in segs:
                                srcv = uTpv[:, kc, pos0:pos0 + n + CW - 1] if kind == "P" else uTsv[:, kc, 0:n + CW - 1]
                                res.append(e.dma_start(out=uin[b][:, o:o + n + CW - 1], in_=srcv))
                                o += n + CW - 1
                            return res
                        S.dma("sp", dmafn(ldu, len(segs)), f"ld_du{b}", writes=[f"d_u{b}"])

                        def ldv(e, b=b, kc=kc):
                            res = []
                            o = 0
                            for (kind, pos0, n) in segs:
                                srcv = uTpv[:, kc, pos0 + 1:pos0 + n + CW - 1] if kind == "P" else uTsv[:, kc, 1:n + CW - 1]
                                res.append(e.dma_start(out=uin2[b][:, o:o + n + CW - 2], in_=srcv))
                                o += n + CW - 1
                            return res
                        S.dma("sp", dmafn(ldv, len(segs)), f"ld_dv{b}", writes=[f"d_v{b}"])
                    for w in range(CW):
                        for b, kc in enumerate(wave):
                            if w == 0:
                                S.op(engs[b], lambda e, b=b, kc=kc: e.tensor_scalar(
                                    out=yb[b][:, 0:LY], in0=uin[b][:, 0:LY], scalar1=dwT[:, kc, 0:1], scalar2=dwbT[:, kc:kc + 1], op0=ALU.mult, op1=ALU.add),
                                    reads=[f"d_u{b}"], writes=[f"d_y{b}"])
                            else:
                                src_t = uin[b][:, w:w + LY] if w % 2 == 0 else uin2[b][:, w - 1:w - 1 + LY]
                                S.op(engs[b], lambda e, b=b, kc=kc, w=w, src_t=src_t: e.scalar_tensor_tensor(
                                    out=yb[b][:, 0:LY], in0=src_t, scalar=dwT[:, kc, w:w + 1], in1=yb[b][:, 0:LY],
                                    op0=ALU.mult, op1=ALU.add), reads=[f"d_u{b}", f"d_v{b}", f"d_y{b}"], writes=[f"d_y{b}"])
                    for b, kc in enumerate(wave):
                        S.op("act", lambda e, b=b: e.activation(out=ysq[b][:, 0:LY], in_=yb[b][:, 0:LY], func=AF.Square), reads=[f"d_y{b}"], writes=[f"d_q{b}"])

                        def st(e, b=b, kc=kc):
                            ins = None
                            for ci, (kind, pos0, n, tok0, acol) in enumerate(chunks):
                                ins = e.matmul(pst[2 * ci][:, :n], lhsT=onesf[:, :], rhs=yb[b][:, ycol[ci]:ycol[ci] + n], start=(kc == 0), stop=(kc == KC - 1))
                                ins = e.matmul(pst[2 * ci + 1][:, :n], lhsT=onesf[:, :], rhs=ysq[b][:, ycol[ci]:ycol[ci] + n], start=(kc == 0), stop=(kc == KC - 1))
                            return ins
                        S.op("pe", st, reads=[f"d_y{b}", f"d_q{b}", "onesf"], writes=["d_psall"])

                        def sty(e, b=b, kc=kc):
                            res = []
                            for ci, (kind, pos0, n, tok0, acol) in enumerate(chunks):
                                res.append(e.dma_start(out=yTv[:, kc, tok0:tok0 + n], in_=yb[b][:, ycol[ci]:ycol[ci] + n]))
                            return res
                        S.dma("sp", dmafn(sty, len(chunks)), f"st_dy{b}", reads=[f"d_y{b}"], writes=[f"yT{kc}"])
                for ci, (kind, pos0, n, tok0, acol) in enumerate(chunks):
                    S.op("dve", lambda e, ci=ci, n=n, acol=acol: e.tensor_scalar(out=mean[:, acol:acol + n], in0=pst[2 * ci][:, :n], scalar1=float(1.0 / D), scalar2=None, op0=ALU.mult),
                         reads=["d_psall"], writes=["d_mean"])
                    S.op("dve", lambda e, ci=ci, n=n, acol=acol: e.tensor_scalar(out=rstd[:, acol:acol + n], in0=pst[2 * ci + 1][:, :n], scalar1=float(1.0 / D), scalar2=None, op0=ALU.mult),
                         reads=["d_psall"], writes=["d_rstd"])
                S.op("dve", lambda e: e.tensor_tensor(out=ysq[0][:, :ntok], in0=mean[:, :ntok], in1=mean[:, :ntok], op=ALU.mult), reads=["d_mean", "d_q0"], writes=["d_q0"])
                S.op("dve", lambda e: e.tensor_tensor(out=rstd[:, :ntok], in0=rstd[:, :ntok], in1=ysq[0][:, :ntok], op=ALU.subtract), reads=["d_rstd", "d_q0"], writes=["d_rstd"])
                S.op("act", lambda e: e.activation(out=rstd[:, :ntok], in_=rstd[:, :ntok], func=AF.Sqrt, bias=float(NORM_EPS)), reads=["d_rstd"], writes=["d_rstd"])
                S.op("dve", lambda e: e.reciprocal(out=rstd[:, :ntok], in_=rstd[:, :ntok]), reads=["d_rstd"], writes=["d_rstd"])
                for kc in range(KC):
                    b = kc % NW

                    def ldy(e, b=b, kc=kc):
                        res = []
                        for (kind, pos0, n, tok0, acol) in chunks:
                            res.append(e.dma_start(out=yb[b][:, acol:acol + n], in_=yTv[:, kc, tok0:tok0 + n]))
                        return res
                    S.dma("sp", dmafn(ldy, len(chunks)), f"ld_dy{b}", reads=[f"yT{kc}"], writes=[f"d_y{b}"])
                    S.op("dve", lambda e, b=b: e.tensor_tensor(out=yb[b][:, :ntok], in0=yb[b][:, :ntok], in1=mean[:, :ntok], op=ALU.subtract), reads=[f"d_y{b}", "d_mean"], writes=[f"d_y{b}"])
                    S.op("dve", lambda e, b=b: e.tensor_tensor(out=yb[b][:, :ntok], in0=yb[b][:, :ntok], in1=rstd[:, :ntok], op=ALU.mult), reads=[f"d_y{b}", "d_rstd"], writes=[f"d_y{b}"])
                    S.op("act", lambda e, b=b, kc=kc: e.activation(out=Abuf[:, kc, 0:ntok], in_=yb[b][:, :ntok], func=AF.Silu, scale=clngT[:, kc:kc + 1], bias=clnbT[:, kc:kc + 1]),
                         reads=[f"d_y{b}"], writes=["Abuf"])
                S.emit("dw")

        def phase_conv_state_init():
            S = Sched(nc)
            with ExitStack() as ps:
                z = ps.enter_context(SBT("z_z", [128, KC, CW - 1], F32))
                sti = ps.enter_context(SBT("z_in", [CW - 1, D], F32))
                stt = ps.enter_context(SBT("z_t", [128, KC, CW - 1], F32))
                pz = ps.enter_context(PST("z_ps", [128, 512], F32))
                S.op("dve", lambda e: e.memset(z[:], 0.0), writes=["z_z"])
                S.dma("sp", dmafn(lambda e: [e.dma_start(out=uTpv[:, k0:k0 + KS, 0:CW - 1], in_=z[:, k0:k0 + KS, :]) for k0 in range(0, KC, KS)], KC // KS), "st_z", reads=["z_z"], writes=[])
                S.dma("sp", dmafn(lambda e: e.dma_start(out=sti[:], in_=st_c[:, :]), 1), "ld_zi", writes=["z_in"])
                for kc in range(KC):
                    S.op("pe", lambda e, kc=kc: e.transpose(out=pz[:, 0:CW - 1], in_=sti[:, kc * 128:(kc + 1) * 128], identity=ident[:CW - 1, :CW - 1]), reads=["z_in", "ident"], writes=["z_ps"])
                    S.op("dve", lambda e, kc=kc: e.tensor_copy(out=stt[:, kc, :], in_=pz[:, 0:CW - 1]), reads=["z_ps"], writes=["z_t"])
                S.dma("sp", dmafn(lambda e: [e.dma_start(out=uTsv[:, k0:k0 + KS, 0:CW - 1], in_=stt[:, k0:k0 + KS, :]) for k0 in range(0, KC, KS)], KC // KS), "st_zt", reads=["z_t"], writes=[])
                S.emit("zi")

        def phase_conv_state_out():
            S = Sched(nc)
            with ExitStack() as ps:
                ui = ps.enter_context(SBT("o_in", [128, KC, CW - 1], F32))
                uo = ps.enter_context(SBT("o_out", [CW - 1, D], F32))
                po = ps.enter_context(PST("o_ps", [128, 512], F32))
                for (srcv, dst) in ((uTpv[:, :, SEQ:SEQ + CW - 1], cv_p), (uTsv[:, :, DEC:DEC + CW - 1], cv_so)):
                    S.dma("sp", dmafn(lambda e, srcv=srcv: [e.dma_start(out=ui[:, k0:k0 + KS, :], in_=srcv[:, k0:k0 + KS, :]) for k0 in range(0, KC, KS)], KC // KS), "ld_oi", writes=["o_in"])
                    for kc in range(KC):
                        S.op("pe", lambda e, kc=kc: e.transpose(out=po[:CW - 1, 0:128], in_=ui[:, kc, :], identity=ident[:, :]), reads=["o_in", "ident"], writes=["o_ps"])
                        S.op("dve", lambda e, kc=kc: e.tensor_copy(out=uo[:, kc * 128:(kc + 1) * 128], in_=po[:CW - 1, 0:128]), reads=["o_ps"], writes=["o_out"])
                    S.dma("sp", dmafn(lambda e, dst=dst: e.dma_start(out=dst[:, :], in_=uo[:]), 1), "st_oo", reads=["o_out"], writes=[])
                S.emit("zo")

        stop_after = cfg.get("stop_after", 99)
        phase_consts()
        phase_mod()
        phase_xin()
        if stop_after >= 2:
            phase_conv_state_init()
            for gi, g in enumerate(groups):
                phase_norm(g, 0, 0)
                phase_qkv(g)
                if stop_after >= 2.5:
                    phase_attn(g)
                if stop_after >= 3:
                    phase_proj_resid(g, w_out, 0, "wo")
                if stop_after >= 4:
                    phase_norm(g, 0, 1)
                    phase_mlp(g, 0)
                if stop_after >= 5:
                    phase_norm(g, 1, 0)
                    phase_conv(g, gi == len(groups) - 1)
                    phase_proj_resid(g, pw2, 1, "p2")
                if stop_after >= 6:
                    phase_norm(g, 1, 1)
                    phase_mlp(g, 1)
                    phase_norm(g, 0, 0, final=True)
            if stop_after >= 5:
                phase_conv_state_out()
    return nc


def make_consts(cfg):
    D, SEQ, DEC, PAST = cfg["D"], cfg["SEQ"], cfg["DEC"], cfg["PAST"]
    half = 64
    inv = np.power(np.float32(10000.0), -np.arange(half, dtype=np.float32) / np.float32(half)).astype(np.float32)
    pos = np.concatenate([np.arange(SEQ), PAST + np.arange(DEC)]).astype(np.float32)
    ang = (pos[:, None] * inv[None, :]).astype(np.float32)
    cos = np.cos(ang).astype(np.float32); sin = np.sin(ang).astype(np.float32)
    q = np.arange(128)[:, None]; k = np.arange(128)[None, :]
    return dict(
        k_id=np.eye(128, dtype=np.float32),
        k_ropeC=np.concatenate([cos, cos], axis=1), k_ropeS=np.concatenate([-sin, sin], axis=1),
        k_mdiff=np.where((k // 64) <= (q // 64), 0.0, -1e30).astype(np.float32),
        k_msb01=(k < q).astype(np.float32),
        k_msbneg=np.where(k < q, 0.0, -1e30).astype(np.float32),
    )


def make_in_maps(cfg, inp, ncores):
    D, SEQ, DEC, PAST = cfg["D"], cfg["SEQ"], cfg["DEC"], cfg["PAST"]
    KC = D // 128
    DH = D // 2
    f = lambda a: np.ascontiguousarray(np.asarray(a, dtype=np.float32))
    cst = make_consts(cfg)
    shared = dict(
        w_mod=f(inp["w_mod"]), b_mod=f(inp["b_mod"]).reshape(2, 6 * KC, 128),
        n_mix=f(inp["norm_mix"]).reshape(2, KC, 128), n_mlp=f(inp["norm_mlp"]).reshape(2, KC, 128),
        w_in=f(inp["w_attn_in"])[0], w_out=f(inp["w_attn_out"])[0],
        lam_v=np.stack([f(inp["lambda_q1"])[0], f(inp["lambda_k1"])[0], f(inp["lambda_q2"])[0], f(inp["lambda_k2"])[0]]),
        subln=f(inp["diff_subln_g"]).reshape(1, 256),
        pw1=f(inp["conv_pw1"])[0], cdw=f(inp["conv_dw"])[0], cdw_b=f(inp["conv_dw_b"]).reshape(KC, 128),
        cln_g=f(inp["conv_ln_g"]).reshape(KC, 128), cln_b=f(inp["conv_ln_b"]).reshape(KC, 128), pw2=f(inp["conv_pw2"])[0],
        m_up=f(inp["mlp_up"]), m_dn=f(inp["mlp_down"]), fin_g=f(inp["final_g"]).reshape(KC, 128),
        **cst,
    )
    maps = []
    for b in range(ncores):
        m = dict(shared)
        m.update(
            x_p=f(inp["x_prompt"][b]), x_s=f(inp["x_sample"][b]),
            c_p=f(inp["c_prompt"][b]).reshape(KC, 128), c_s=f(inp["c_sample"][b]).reshape(KC, 128),
            ck_d=f(inp["cache_k_diff"][0, b]).reshape(PAST, -1), cv_d=f(inp["cache_v_diff"][0, b]).reshape(PAST, -1),
            ck_s=f(inp["cache_k_sb"][0, b]).reshape(PAST, -1), cv_s=f(inp["cache_v_sb"][0, b]).reshape(PAST, -1),
            st_c=f(inp["state_conv"][0, b]),
        )
        maps.append(m)
    return maps


def assemble(cfg, results):
    D, SEQ, DEC = cfg["D"], cfg["SEQ"], cfg["DEC"]
    HA = D // 512; HB = D // 256
    st = lambda k: np.stack([np.asarray(r[k], dtype=np.float32) for r in results])
    B = len(results)
    return (
        st("y_p"), st("y_s"),
        st("kd_p").reshape(1, B, SEQ, 2 * HA, 128), st("vd_p").reshape(1, B, SEQ, HA, 256),
        st("ks_p").reshape(1, B, SEQ, HB, 128), st("vs_p").reshape(1, B, SEQ, HB, 128),
        st("cv_p").reshape(1, B, CW - 1, D),
        st("kd_s").reshape(1, B, DEC, 2 * HA, 128), st("vd_s").reshape(1, B, DEC, HA, 256),
        st("ks_s").reshape(1, B, DEC, HB, 128), st("vs_s").reshape(1, B, DEC, HB, 128),
        st("cv_s_o").reshape(1, B, CW - 1, D),
    )


def kernel(**inputs):
    cfg = FULL
    nc = build(cfg)
    maps = make_in_maps(cfg, inputs, 8)
    res = run_bass_kernel_spmd(nc, maps, core_ids=list(range(8)))
    return assemble(cfg, res.results)
```

```python
import math
from contextlib import ExitStack

import numpy as np
import concourse.bass as bass
import concourse.mybir as mybir
from concourse.bass_utils import run_bass_kernel_spmd

F32 = mybir.dt.float32
BF16 = mybir.dt.bfloat16
AF = mybir.ActivationFunctionType
ALU = mybir.AluOpType
AX = mybir.AxisListType

FULL = dict(D=4096, SEQ=2048, DEC=64, PAST=2048, CH=512)
NORM_EPS = 1e-6
SUBLN_EPS = 1e-5
CW = 31


class Sched:
    ENG = ("pe", "act", "dve", "pool", "sp")

    def __init__(self, nc):
        self.nc = nc
        self.ops = []
        self.last_w = {}
        self.readers = {}

    dry = False

    def op(self, eng, fn, reads=(), writes=(), slot=None):
        if self.dry:
            return -1
        deps = set()
        for r in reads:
            if r in self.last_w:
                deps.add(self.last_w[r])
        for w in writes:
            if w in self.last_w:
                deps.add(self.last_w[w])
            for rd in self.readers.get(w, ()):
                deps.add(rd)
        idx = len(self.ops)
        self.ops.append(dict(eng=eng, fn=fn, deps=deps, slot=slot, used=False))
        for d in deps:
            self.ops[d]["used"] = True
        for w in writes:
            self.last_w[w] = idx
            self.readers[w] = []
        for r in reads:
            if r not in writes:
                self.readers.setdefault(r, []).append(idx)
        return idx

    def dma(self, eng, fn, slot, reads=(), writes=()):
        return self.op(eng, fn, reads, writes, slot=slot)

    _n = [0]

    pool_stack = None
    esem = {}
    ssem = {}
    ecount = {}
    scount = {}

    @classmethod
    def reset_pool(cls, stack):
        cls.pool_stack = stack
        cls.esem = {}
        cls.ssem = {}
        cls.ecount = {}
        cls.scount = {}

    def emit(self, name="blk"):
        nc = self.nc
        ops = self.ops
        if not ops:
            return
        cls = Sched
        for e in self.ENG:
            if e not in cls.esem:
                cls.esem[e] = cls.pool_stack.enter_context(nc.semaphore(f"sem_e_{e}"))
                cls.ecount[e] = 0
        for o in ops:
            sl = o["slot"]
            if sl is not None and sl not in cls.ssem:
                cls.ssem[sl] = cls.pool_stack.enter_context(nc.semaphore(f"sem_s_{sl}"))
                cls.scount[sl] = 0
        esem, ssem, ecount, scount = cls.esem, cls.ssem, cls.ecount, cls.scount
        with ExitStack() as st:
            block = st.enter_context(nc.Block())
            handles = {}

            def run_engine(engname, e):
                waited = {}
                mine = [i for i, o in enumerate(ops) if o["eng"] == engname]
                last_dma = {}
                for i in mine:
                    o = ops[i]
                    for d in sorted(o["deps"]):
                        p = ops[d]
                        if p["eng"] == engname and p["slot"] is None and engname == "pe":
                            continue
                        sem, val = p["sig"]
                        if waited.get(id(sem), 0) >= val:
                            continue
                        e.wait_ge(sem, val)
                        waited[id(sem)] = val
                    res = o["fn"](e)
                    if o["slot"] is not None:
                        sem, val = o["sig"]
                        lst = res if isinstance(res, (list, tuple)) else [res]
                        for ins in lst:
                            ins.then_inc(sem, 16)
                        last_dma[o["slot"]] = (sem, val)
                    else:
                        sem, val = o["sig"]
                        res.then_inc(sem, 1)
                for s, (sem, val) in last_dma.items():
                    if waited.get(id(sem), 0) < val:
                        e.wait_ge(sem, val)

            for o in ops:
                if o["slot"] is not None:
                    n = getattr(o["fn"], "ndma", 1)
                    scount[o["slot"]] += 16 * n
                    o["sig"] = (ssem[o["slot"]], scount[o["slot"]])
                else:
                    ecount[o["eng"]] += 1
                    o["sig"] = (esem[o["eng"]], ecount[o["eng"]])

            used = {o["eng"] for o in ops}
            if "pe" in used:
                @block.tensor
                def _(e):
                    run_engine("pe", e)
            if "act" in used:
                @block.scalar
                def _(e):
                    run_engine("act", e)
            if "dve" in used:
                @block.vector
                def _(e):
                    run_engine("dve", e)
            if "pool" in used:
                @block.gpsimd
                def _(e):
                    run_engine("pool", e)
            if "sp" in used:
                @block.sync
                def _(e):
                    run_engine("sp", e)
        self.ops = []
        self.last_w = {}
        self.readers = {}


def dmafn(fn, n):
    fn.ndma = n
    return fn


def build(cfg, debug_outs=()):
    D, SEQ, DEC, PAST, CH = cfg["D"], cfg["SEQ"], cfg["DEC"], cfg["PAST"], cfg["CH"]
    KC = D // 128
    KS = min(8, KC)
    HA = D // 512
    NQA = 2 * HA
    HB = D // 256
    DH = D // 2
    QKV = 3 * D
    DFF = 4 * D
    TT = SEQ + DEC
    BWQ = min(512, DH)
    NPT = SEQ // 128
    lam_init = 0.8 - 0.6 * math.exp(-0.3 * 0)

    nc = bass.Bass("TRN2", target_bir_lowering=False)
    uid = [0]

    def SBT(name, shape, dt):
        uid[0] += 1
        return nc.sbuf_tensor(f"{name}_{uid[0]}", shape, dt)

    def PST(name, shape, dt):
        uid[0] += 1
        return nc.psum_tensor(f"{name}_{uid[0]}", shape, dt)

    def din(name, shape, dt=F32):
        return nc.dram_tensor(name, list(shape), dt, kind="ExternalInput").ap()

    def dout(name, shape, dt=F32):
        return nc.dram_tensor(name, list(shape), dt, kind="ExternalOutput").ap()

    def dscr(name, shape, dt=F32):
        kind = "ExternalOutput" if name in debug_outs else "Internal"
        return nc.dram_tensor(name, list(shape), dt, kind=kind).ap()

    x_p = din("x_p", [SEQ, D]); x_s = din("x_s", [DEC, D])
    c_p = din("c_p", [KC, 128]); c_s = din("c_s", [KC, 128])
    ck_d = din("ck_d", [PAST, NQA * 128]); cv_d = din("cv_d", [PAST, DH])
    ck_s = din("ck_s", [PAST, DH]); cv_s = din("cv_s", [PAST, DH])
    st_c = din("st_c", [CW - 1, D])
    w_mod = din("w_mod", [2, D, 6 * D]); b_mod = din("b_mod", [2, 6 * KC, 128])
    n_mix = din("n_mix", [2, KC, 128]); n_mlp = din("n_mlp", [2, KC, 128])
    w_in = din("w_in", [D, QKV]); w_out = din("w_out", [D, D])
    lam_v = din("lam_v", [4, 128]); subln = din("subln", [1, 256])
    pw1 = din("pw1", [D, 2 * D]); cdw = din("cdw", [CW, D])
    cdw_b = din("cdw_b", [KC, 128]); cln_g = din("cln_g", [KC, 128]); cln_b = din("cln_b", [KC, 128])
    pw2 = din("pw2", [D, D])
    m_up = din("m_up", [2, D, DFF]); m_dn = din("m_dn", [2, DFF, D])
    fin_g = din("fin_g", [KC, 128])
    k_id = din("k_id", [128, 128]); k_ropeC = din("k_ropeC", [TT, 128]); k_ropeS = din("k_ropeS", [TT, 128])
    k_mdiff = din("k_mdiff", [128, 128]); k_msb01 = din("k_msb01", [128, 128]); k_msbneg = din("k_msbneg", [128, 128])

    y_p = dout("y_p", [SEQ, D]); y_s = dout("y_s", [DEC, D])
    kd_p = dout("kd_p", [SEQ, DH]); vd_p = dout("vd_p", [SEQ, DH]); ks_p = dout("ks_p", [SEQ, DH]); vs_p = dout("vs_p", [SEQ, DH])
    cv_p = dout("cv_p", [CW - 1, D])
    kd_s = dout("kd_s", [DEC, DH]); vd_s = dout("vd_s", [DEC, DH]); ks_s = dout("ks_s", [DEC, DH]); vs_s = dout("vs_s", [DEC, DH])
    cv_so = dout("cv_s_o", [CW - 1, D])

    xT = dscr("xT", [D, TT])
    qT = dscr("qT", [NQA + HB, 128, TT], BF16)
    kT = dscr("kT", [NQA + HB, 128, TT], BF16)
    vbf = dscr("vbf", [TT, D], BF16)
    gT = dscr("gT", [DFF, TT], BF16)
    uTp = dscr("uTp", [D, CW - 1 + SEQ])
    uTs = dscr("uTs", [D, CW - 1 + DEC])
    yT = dscr("yT", [D, TT])
    xTv = xT.rearrange("(k p) t -> p k t", p=128)
    gTv = gT.rearrange("(k p) t -> p k t", p=128)
    uTpv = uTp.rearrange("(k p) t -> p k t", p=128)
    uTsv = uTs.rearrange("(k p) t -> p k t", p=128)
    yTv = yT.rearrange("(k p) t -> p k t", p=128)

    groups = []
    npc = SEQ // CH
    half = max(1, npc // 2)
    g0 = [("P", i * CH, CH, i * CH) for i in range(half)] + [("S", 0, DEC, SEQ)]
    g1 = [("P", i * CH, CH, i * CH) for i in range(half, npc)]
    groups = [g0] + ([g1] if g1 else [])
    GMAX = max(sum(c[2] for c in g) for g in groups)

    def tiles_of(group):
        out = []
        col = 0
        for (kind, pos0, n, tok0) in group:
            for o in range(0, n, 128):
                m = min(128, n - o)
                out.append((kind, pos0 + o, m, tok0 + o, col + o))
            col += n
        return out

    def chunks_of(group):
        out = []
        col = 0
        for (kind, pos0, n, tok0) in group:
            out.append((kind, pos0, n, tok0, col))
            col += n
        return out

    es = ExitStack()
    with es:
        Sched.reset_pool(es)

        def sb(name, shape, dt=F32):
            return es.enter_context(SBT(name, list(shape), dt))

        Abuf = sb("Abuf", [128, KC, GMAX], BF16)
        Wb = [None, None]

        def alloc_W(ps):
            for i in range(2):
                Wb[i] = ps.enter_context(SBT(f"Wb{i}", [128, KC, 512], BF16))
        ident = sb("ident", [128, 128]); identb = sb("identb", [128, 128], BF16)
        onesf = sb("onesf", [128, 128])
        mdiff = sb("mdiff", [128, 128]); msb01 = sb("msb01", [128, 128]); msbneg = sb("msbneg", [128, 128])
        modT = sb("modT", [128, 2, 6, KC, 2])
        bmodT = sb("bmodT", [128, 2, 6 * KC])
        nmixT = sb("nmixT", [128, 2, KC]); nmlpT = sb("nmlpT", [128, 2, KC]); fingT = sb("fingT", [128, KC])
        dwT = sb("dwT", [128, KC, CW]); dwbT = sb("dwbT", [128, KC]); clngT = sb("clngT", [128, KC]); clnbT = sb("clnbT", [128, KC])
        csT = sb("csT", [128, KC, 2], BF16)
        sgbc = sb("sgbc", [128, 256])
        lamt = sb("lamt", [128, 4])
        PA = sb("PA", [128, 2, 2, 2, KC]); PB = sb("PB", [128, 2, 2, 2, KC]); PG = sb("PG", [128, 2, 2, 2, KC])

        KI = {"P": 0, "S": 1}

        def phase_consts():
            S = Sched(nc)
            with ExitStack() as ps:
                def t(name, shape, dt=F32):
                    return ps.enter_context(SBT(name, list(shape), dt))
                pt = ps.enter_context(PST("c_pt", [128, 512], F32))
                S.dma("sp", dmafn(lambda e: e.dma_start(out=ident[:], in_=k_id[:, :]), 1), "ld_id", writes=["ident"])
                S.dma("sp", dmafn(lambda e: e.dma_start(out=mdiff[:], in_=k_mdiff[:, :]), 1), "ld_m1", writes=["mdiff"])
                S.dma("sp", dmafn(lambda e: e.dma_start(out=msb01[:], in_=k_msb01[:, :]), 1), "ld_m2", writes=["msb01"])
                S.dma("sp", dmafn(lambda e: e.dma_start(out=msbneg[:], in_=k_msbneg[:, :]), 1), "ld_m3", writes=["msbneg"])
                S.op("dve", lambda e: e.tensor_copy(out=identb[:], in_=ident[:]), reads=["ident"], writes=["identb"])
                S.op("dve", lambda e: e.memset(onesf[:], 1.0), writes=["onesf"])
                S.dma("sp", dmafn(lambda e: e.dma_start(out=sgbc[:], in_=subln[0:1, :].broadcast_to([128, 256])), 1), "ld_sg", writes=["sgbc"])
                S.op("dve", lambda e: e.tensor_scalar(out=sgbc[:], in0=sgbc[:], scalar1=float(1.0 - lam_init), scalar2=None, op0=ALU.mult),
                     reads=["sgbc"], writes=["sgbc"])
                lv = t("lv", [128, 4, 128])
                S.dma("sp", dmafn(lambda e: e.dma_start(out=lv[:], in_=lam_v.rearrange("(o a) b -> o a b", o=1).broadcast_to([128, 4, 128])), 1), "ld_lv", writes=["lv"])
                lp = t("lp", [128, 2, 128]); ls = t("ls", [128, 2]); le = t("le", [128, 2])
                S.op("dve", lambda e: e.tensor_tensor(out=lp[:, 0, :], in0=lv[:, 0, :], in1=lv[:, 1, :], op=ALU.mult), reads=["lv"], writes=["lp0"])
                S.op("dve", lambda e: e.tensor_tensor(out=lp[:, 1, :], in0=lv[:, 2, :], in1=lv[:, 3, :], op=ALU.mult), reads=["lv"], writes=["lp1"])
                S.op("dve", lambda e: e.tensor_reduce(out=ls[:], in_=lp[:], axis=AX.X, op=ALU.add), reads=["lp0", "lp1"], writes=["ls"])
                S.op("act", lambda e: e.activation(out=le[:], in_=ls[:], func=AF.Exp), reads=["ls"], writes=["le"])
                S.op("dve", lambda e: e.tensor_tensor(out=lamt[:, 0:1], in0=le[:, 1:2], in1=le[:, 0:1], op=ALU.subtract), reads=["le"], writes=["lamt"])
                S.op("dve", lambda e: e.tensor_scalar(out=lamt[:, 0:1], in0=lamt[:, 0:1], scalar1=float(-lam_init), scalar2=None, op0=ALU.add),
                     reads=["lamt"], writes=["lamt"])

                stage = t("vstage", [128, 128])
                cnt = [0]

                def vecT(src_rows_ap, nrows, dst_ap, func=None):
                    k = cnt[0]; cnt[0] += 1
                    S.dma("sp", dmafn(lambda e: e.dma_start(out=stage[:nrows, :], in_=src_rows_ap), 1), "ld_vs", writes=["vstage"])
                    S.op("pe", lambda e: e.transpose(out=pt[:, :nrows], in_=stage[:nrows, :], identity=ident[:nrows, :nrows]),
                         reads=["vstage", "ident"], writes=["c_pt"])
                    if func is None:
                        S.op("dve", lambda e: e.tensor_copy(out=dst_ap, in_=pt[:, :nrows]), reads=["c_pt"], writes=["vec%d" % k])
                    else:
                        S.op("act", lambda e: e.activation(out=dst_ap, in_=pt[:, :nrows], func=func), reads=["c_pt"], writes=["vec%d" % k])

                for l in range(2):
                    vecT(n_mix[l], KC, nmixT[:, l, :]); vecT(n_mlp[l], KC, nmlpT[:, l, :])
                    for r0 in range(0, 6 * KC, 128):
                        nr = min(128, 6 * KC - r0)
                        vecT(b_mod[l, r0:r0 + nr, :], nr, bmodT[:, l, r0:r0 + nr])
                vecT(fin_g, KC, fingT[:]); vecT(cdw_b, KC, dwbT[:]); vecT(cln_g, KC, clngT[:]); vecT(cln_b, KC, clnbT[:])
                for kc in range(KC):
                    k = cnt[0]; cnt[0] += 1
                    S.dma("sp", dmafn(lambda e, kc=kc: e.dma_start(out=stage[:CW, :], in_=cdw[:, kc * 128:(kc + 1) * 128]), 1), "ld_vs", writes=["vstage"])
                    S.op("pe", lambda e: e.transpose(out=pt[:, :CW], in_=stage[:CW, :], identity=ident[:CW, :CW]), reads=["vstage", "ident"], writes=["c_pt"])
                    S.op("dve", lambda e, kc=kc: e.tensor_copy(out=dwT[:, kc, :], in_=pt[:, :CW]), reads=["c_pt"], writes=["vec%d" % k])
                vecT(c_p, KC, csT[:, :, 0], func=AF.Silu)
                vecT(c_s, KC, csT[:, :, 1], func=AF.Silu)
                S.emit("c0")

        wstate = {"n": 0}

        def load_w(S, src_ap_fn, width=512):
            slot = wstate["n"] % 2
            wstate["n"] += 1
            nsp = 4 if KC >= 4 else 1
            step = KC // nsp

            def fn(e, slot=slot):
                res = []
                for q in range(nsp):
                    res.append(e.dma_start(out=Wb[slot][:, q * step:(q + 1) * step, 0:width], in_=src_ap_fn(q * step, (q + 1) * step)))
                return res
            S.dma("pool", dmafn(fn, nsp), f"ldW{slot}", writes=[f"W{slot}"])
            return slot

        def wsrc(w2d, r0, c0, width):
            v = w2d[r0:r0 + D, c0:c0 + width].rearrange("(k p) n -> p k n", p=128)
            return lambda a, b: v[:, a:b, :]

        def phase_mod():
            S = Sched(nc)
            with ExitStack() as ps:
                alloc_W(ps)
                pm = [ps.enter_context(PST(f"m_ps{i}", [128, 512], F32)) for i in range(2)]
                nblk = 6 * D // 512
                for l in range(2):
                    pend = None
                    slots = {}
                    slots[0] = load_w(S, wsrc(w_mod[l], 0, 0, 512))
                    for j in range(nblk):
                        if j + 1 < nblk:
                            slots[j + 1] = load_w(S, wsrc(w_mod[l], 0, (j + 1) * 512, 512))
                        slot = slots[j]
                        pj = pm[j % 2]

                        def mm(e, slot=slot, pj=pj):
                            ins = None
                            for s in range(4):
                                for kc in range(KC):
                                    ins = e.matmul(pj[:, 2 * s:2 * s + 2], lhsT=Wb[slot][:, kc, s * 128:(s + 1) * 128], rhs=csT[:, kc, :],
                                                   start=(kc == 0), stop=(kc == KC - 1))
                            return ins
                        S.op("pe", mm, reads=[f"W{slot}", "csT"], writes=[f"m_ps{j % 2}"])
                        c0 = j * 4
                        which, kc0 = c0 // KC, c0 % KC

                        def ev(e, l=l, which=which, kc0=kc0, pj=pj, c0=c0):
                            return e.tensor_tensor(out=modT[:, l, which, kc0:kc0 + 4, :],
                                                   in0=pj[:, 0:8].rearrange("p (s k) -> p s k", k=2),
                                                   in1=bmodT[:, l, c0:c0 + 4].unsqueeze(2).broadcast_to([128, 4, 2]), op=ALU.add)
                        S.op("dve", ev, reads=[f"m_ps{j % 2}"], writes=["modT"])
                sqD = float(math.sqrt(D))
                for l in range(2):
                    for mf in range(2):
                        gain = (nmixT if mf == 0 else nmlpT)[:, l, :]
                        for kind in range(2):
                            sh = modT[:, l, 3 * mf + 0, :, kind]; sc = modT[:, l, 3 * mf + 1, :, kind]; gt = modT[:, l, 3 * mf + 2, :, kind]
                            S.op("dve", lambda e, sc=sc, gain=gain, l=l, mf=mf, kind=kind: e.scalar_tensor_tensor(
                                out=PA[:, l, mf, kind, :], in0=sc, scalar=1.0, in1=gain, op0=ALU.add, op1=ALU.mult), reads=["modT"], writes=["PAt"])
                            S.op("dve", lambda e, sh=sh, l=l, mf=mf, kind=kind: e.tensor_copy(out=PB[:, l, mf, kind, :], in_=sh), reads=["modT"], writes=["PBt"])
                            S.op("dve", lambda e, gt=gt, l=l, mf=mf, kind=kind: e.tensor_copy(out=PG[:, l, mf, kind, :], in_=gt), reads=["modT"], writes=["PGt"])
                S.emit("md")

        def phase_xin():
            S = Sched(nc)
            with ExitStack() as ps:
                xin = [ps.enter_context(SBT(f"xin{i}", [128, D], F32)) for i in range(2)]
                xst = [ps.enter_context(SBT(f"xst{i}", [128, KC, 128], F32)) for i in range(1)] * 2
                pp = [ps.enter_context(PST(f"x_ps{i}", [128, 512], F32)) for i in range(2)]
                tl = [("P", i * 128, 128, i * 128) for i in range(NPT)] + [("S", 0, DEC, SEQ)]
                nb = 0
                for ti, (kind, pos0, n, tok0) in enumerate(tl):
                    b = ti % 2
                    src = x_p[pos0:pos0 + n, :] if kind == "P" else x_s[0:n, :]
                    S.dma("sp", dmafn(lambda e, b=b, n=n, src=src: e.dma_start(out=xin[b][:n, :], in_=src), 1), f"ld_xin{b}", writes=[f"xin{b}"])
                    for k0 in range(0, KC, 4):
                        kk = min(4, KC - k0)
                        p = pp[nb % 2]

                        def tr(e, b=b, n=n, k0=k0, kk=kk, p=p):
                            ins = None
                            for q in range(kk):
                                ins = e.transpose(out=p[:, q * 128:q * 128 + n], in_=xin[b][:n, (k0 + q) * 128:(k0 + q + 1) * 128], identity=ident[:n, :n])
                            return ins
                        S.op("pe", tr, reads=[f"xin{b}", "ident"], writes=[f"x_ps{nb % 2}"])
                        eng = "act" if nb % 2 == 0 else "dve"
                        if eng == "act":
                            S.op("act", lambda e, b=b, n=n, k0=k0, kk=kk, p=p: e.activation(
                                out=xst[b][:, k0:k0 + kk, :n], in_=p[:, 0:kk * 128].rearrange("p (k t) -> p k t", t=128)[:, :, :n], func=AF.Copy),
                                reads=[f"x_ps{nb % 2}"], writes=["xst0"])
                        else:
                            S.op("dve", lambda e, b=b, n=n, k0=k0, kk=kk, p=p: e.tensor_copy(
                                out=xst[b][:, k0:k0 + kk, :n], in_=p[:, 0:kk * 128].rearrange("p (k t) -> p k t", t=128)[:, :, :n]),
                                reads=[f"x_ps{nb % 2}"], writes=["xst0"])
                        nb += 1
                    def stx(e, n=n, tok0=tok0):
                        return [e.dma_start(out=xTv[:, k0:k0 + KS, tok0:tok0 + n], in_=xst[0][:, k0:k0 + KS, :n]) for k0 in range(0, KC, KS)]
                    S.dma("sp", dmafn(stx, KC // KS), "st_xst0", reads=["xst0"], writes=[])
                S.emit("xi")

        def phase_norm(group, l, mf, final=False):
            S = Sched(nc)
            with ExitStack() as ps:
                xt = [ps.enter_context(SBT(f"n_xt{i}", [128, KC, 128], F32)) for i in range(2)]
                sq2 = [ps.enter_context(SBT(f"n_sq{i}", [128, KC, 128], F32)) for i in range(2)]
                rstd2 = [ps.enter_context(SBT(f"n_rstd{i}", [128, 128], F32)) for i in range(2)]
                pss2 = [ps.enter_context(PST(f"n_ps{i}", [128, 128], F32)) for i in range(2)]
                if final:
                    ptr = [ps.enter_context(PST(f"n_pt{i}", [128, 512], F32)) for i in range(2)]
                    yst = [ps.enter_context(SBT(f"n_yst{i}", [128, D], F32)) for i in range(2)]
                nb = 0
                for ti, (kind, pos0, n, tok0, acol) in enumerate(tiles_of(group)):
                    b = ti % 2
                    sq = sq2[ti % len(sq2)]
                    sqr = f"n_sq{ti % len(sq2)}"
                    ki = KI[kind]

                    def ldx(e, b=b, n=n, tok0=tok0):
                        return [e.dma_start(out=xt[b][:, k0:k0 + KS, :n], in_=xTv[:, k0:k0 + KS, tok0:tok0 + n]) for k0 in range(0, KC, KS)]
                    S.dma("sp", dmafn(ldx, KC // KS), f"ld_nxt{b}", writes=[f"n_xt{b}"])
                    S.op("act", lambda e, sq=sq, b=b, n=n: e.activation(out=sq[:, :, :n], in_=xt[b][:, :, :n], func=AF.Square), reads=[f"n_xt{b}"], writes=[sqr])
                    rstd = rstd2[b]; pss = pss2[b]; rsr = f"n_rstd{b}"; psr = f"n_ps{b}"

                    def ssum(e, sq=sq, n=n, pss=pss):
                        ins = None
                        for kc in range(KC):
                            ins = e.matmul(pss[:, :n], lhsT=onesf[:, :], rhs=sq[:, kc, :n], start=(kc == 0), stop=(kc == KC - 1))
                        return ins
                    S.op("pe", ssum, reads=[sqr, "onesf"], writes=[psr])
                    S.op("act", lambda e, n=n, rstd=rstd, pss=pss: e.activation(out=rstd[:, :n], in_=pss[:, :n], func=AF.Sqrt, scale=float(1.0 / D), bias=float(NORM_EPS)),
                         reads=[psr], writes=[rsr])
                    S.op("dve", lambda e, n=n, rstd=rstd: e.reciprocal(out=rstd[:, :n], in_=rstd[:, :n]), reads=[rsr], writes=[rsr])
                    S.op("dve", lambda e, sq=sq, b=b, n=n, rstd=rstd: e.tensor_tensor(out=sq[:, :, :n], in0=xt[b][:, :, :n], in1=rstd[:, :n].unsqueeze(1).broadcast_to([128, KC, n]), op=ALU.mult),
                         reads=[f"n_xt{b}", rsr, sqr], writes=[sqr])
                    if not final:
                        A_ap = PA[:, l, mf, ki, :]; B_ap = PB[:, l, mf, ki, :]
                        S.op("dve", lambda e, sq=sq, n=n, A_ap=A_ap: e.tensor_tensor(out=sq[:, :, :n], in0=sq[:, :, :n], in1=A_ap.unsqueeze(2).broadcast_to([128, KC, n]), op=ALU.mult),
                             reads=[sqr, "PAt"], writes=[sqr])
                        S.op("pool", lambda e, n=n, B_ap=B_ap, acol=acol, sq=sq: e.tensor_tensor(out=Abuf[:, :, acol:acol + n], in0=sq[:, :, :n], in1=B_ap.unsqueeze(2).broadcast_to([128, KC, n]), op=ALU.add),
                             reads=[sqr, "PBt"], writes=["Abuf"])
                    else:
                        S.op("dve", lambda e, sq=sq, n=n: e.tensor_tensor(out=sq[:, :, :n], in0=sq[:, :, :n], in1=fingT[:, :].unsqueeze(2).broadcast_to([128, KC, n]), op=ALU.mult),
                             reads=[sqr], writes=[sqr])
                        for k0 in range(0, KC, 4):
                            kk = min(4, KC - k0)
                            p = ptr[nb % 2]

                            def tr(e, n=n, k0=k0, kk=kk, p=p, sq=sq):
                                ins = None
                                for q in range(kk):
                                    ins = e.transpose(out=p[:n, q * 128:(q + 1) * 128], in_=sq[:, k0 + q, :n], identity=ident[:, :])
                                return ins
                            S.op("pe", tr, reads=[sqr, "ident"], writes=[f"n_pt{nb % 2}"])
                            if nb % 2 == 0:
                                S.op("act", lambda e, b=b, n=n, k0=k0, kk=kk, p=p: e.activation(out=yst[b][:n, k0 * 128:(k0 + kk) * 128], in_=p[:n, 0:kk * 128], func=AF.Copy),
                                     reads=[f"n_pt{nb % 2}"], writes=[f"n_yst{b}"])
                            else:
                                S.op("dve", lambda e, b=b, n=n, k0=k0, kk=kk, p=p: e.tensor_copy(out=yst[b][:n, k0 * 128:(k0 + kk) * 128], in_=p[:n, 0:kk * 128]),
                                     reads=[f"n_pt{nb % 2}"], writes=[f"n_yst{b}"])
                            nb += 1
                        dst = y_p[pos0:pos0 + n, :] if kind == "P" else y_s[0:n, :]
                        S.dma("sp", dmafn(lambda e, b=b, n=n, dst=dst: e.dma_start(out=dst, in_=yst[b][:n, :]), 1), f"st_yst{b}", reads=[f"n_yst{b}"], writes=[])
                S.emit("nm")

        def dense_ws(S, group, nblk, wsrc_fn, epilogue, pst, pair=False):
            chunks = chunks_of(group)
            nchunk = len(chunks)
            slots = {0: load_w(S, wsrc_fn(0))}
            cnt = 0
            for j in range(nblk):
                if j + 1 < nblk:
                    slots[j + 1] = load_w(S, wsrc_fn(j + 1))
                slot = slots[j]
                nsub = 2 if pair else 4
                for s in range(nsub):
                    if not pair:
                        banks = [(cnt * nchunk + ci) % len(pst) for ci in range(nchunk)]
                        cnt += 1

                        def mm(e, slot=slot, s=s, banks=banks):
                            ins = None
                            for kc in range(KC):
                                for ci, (kind, pos0, n, tok0, acol) in enumerate(chunks):
                                    ins = e.matmul(pst[banks[ci]][:, :n], lhsT=Wb[slot][:, kc, s * 128:(s + 1) * 128], rhs=Abuf[:, kc, acol:acol + n],
                                                   start=(kc == 0), stop=(kc == KC - 1))
                            return ins
                        S.op("pe", mm, reads=[f"W{slot}", "Abuf"], writes=[f"ps{b_}" for b_ in banks])
                        for ci, ch in enumerate(chunks):
                            epilogue(S, j, s, ci, ch, pst[banks[ci]], f"ps{banks[ci]}")
                    else:
                        for ci, ch in enumerate(chunks):
                            (kind, pos0, n, tok0, acol) = ch
                            ba = (cnt * 2) % len(pst); bb = (cnt * 2 + 1) % len(pst)
                            cnt += 1

                            def mm(e, slot=slot, s=s, ba=ba, bb=bb, n=n, acol=acol):
                                ins = None
                                for kc in range(KC):
                                    ins = e.matmul(pst[ba][:, :n], lhsT=Wb[slot][:, kc, s * 128:(s + 1) * 128], rhs=Abuf[:, kc, acol:acol + n],
                                                   start=(kc == 0), stop=(kc == KC - 1))
                                for kc in range(KC):
                                    ins = e.matmul(pst[bb][:, :n], lhsT=Wb[slot][:, kc, 256 + s * 128:256 + (s + 1) * 128], rhs=Abuf[:, kc, acol:acol + n],
                                                   start=(kc == 0), stop=(kc == KC - 1))
                                return ins
                            S.op("pe", mm, reads=[f"W{slot}", "Abuf"], writes=[f"ps{ba}", f"ps{bb}"])
                            epilogue(S, j, s, ci, ch, (pst[ba], pst[bb]), (f"ps{ba}", f"ps{bb}"))

        def gated_residual_epilogue(stg, l, mf):
            cnt = [0]

            def ep(S, j, s, ci, ch, p, pname):
                (kind, pos0, n, tok0, acol) = ch
                ko = j * 4 + s
                b = cnt[0] % len(stg); cnt[0] += 1
                g_ap = PG[:, l, mf, KI[kind], ko:ko + 1]
                S.op("act", lambda e, b=b, n=n, p=p, g_ap=g_ap: e.activation(out=stg[b][:, :n], in_=p[:, :n], func=AF.Copy, scale=g_ap),
                     reads=[pname, "PGt"], writes=[f"stg{b}"])
                S.dma("pool", dmafn(lambda e, b=b, n=n, ko=ko, tok0=tok0: e.dma_start(out=xTv[:, ko, tok0:tok0 + n], in_=stg[b][:, :n], accum_op=ALU.add), 1),
                      f"st_stg{b}", reads=[f"stg{b}"], writes=[])
            return ep

        def phase_qkv(group):
            S = Sched(nc)
            tl = tiles_of(group)
            nt = len(tl)
            scale = 1.0 / math.sqrt(128.0)
            with ExitStack() as ps:
                alloc_W(ps)
                pq = [ps.enter_context(PST(f"q_ps{i}", [128, 512], F32)) for i in range(4)]
                ptb = [ps.enter_context(PST(f"q_pt{i}", [128, 512], BF16)) for i in range(2)]
                stg = [ps.enter_context(SBT(f"q_stg{i}", [128, 512], F32)) for i in range(3)]
                t1 = ps.enter_context(SBT("q_t1", [128, 512], F32))
                b16 = [ps.enter_context(SBT(f"q_b16{i}", [128, 512], BF16)) for i in range(2)]
                tst = [ps.enter_context(SBT(f"q_tst{i}", [128, 4, 128], BF16)) for i in range(2)]
                rC = ps.enter_context(SBT("q_rC", [128, nt, 128], F32))
                rS = ps.enter_context(SBT("q_rS", [128, nt, 128], F32))
                for ti, (kind, pos0, n, tok0, acol) in enumerate(tl):
                    S.dma("sp", dmafn(lambda e, ti=ti, n=n, tok0=tok0: e.dma_start(out=rC[:n, ti, :], in_=k_ropeC[tok0:tok0 + n, :]), 1), "ld_rC", writes=["q_rC"])
                    S.dma("sp", dmafn(lambda e, ti=ti, n=n, tok0=tok0: e.dma_start(out=rS[:n, ti, :], in_=k_ropeS[tok0:tok0 + n, :]), 1), "ld_rS", writes=["q_rS"])
                nbs = DH // BWQ
                nh = BWQ // 128
                nblk = 6 * nbs
                outs = {("P", 1): kd_p, ("P", 2): vd_p, ("P", 4): ks_p, ("P", 5): vs_p,
                        ("S", 1): kd_s, ("S", 2): vd_s, ("S", 4): ks_s, ("S", 5): vs_s}

                def src_fn(j):
                    sec, jj = j // nbs, j % nbs
                    return wsrc(w_in, 0, sec * DH + jj * BWQ, BWQ)
                slots = {0: load_w(S, src_fn(0), BWQ)}
                cnt = 0
                pending = []
                for j in range(nblk):
                    if j + 1 < nblk:
                        slots[j + 1] = load_w(S, src_fn(j + 1), BWQ)
                    slot = slots[j]
                    sec, jj = j // nbs, j % nbs
                    for ti, (kind, pos0, n, tok0, acol) in enumerate(tl):
                        pb = cnt % 4; sg = cnt % 3; bb = cnt % 2
                        cnt += 1
                        p = pq[pb]

                        def mm(e, slot=slot, n=n, acol=acol, p=p):
                            ins = None
                            for kc in range(KC):
                                ins = e.matmul(p[:n, :BWQ], lhsT=Abuf[:, kc, acol:acol + n], rhs=Wb[slot][:, kc, 0:BWQ], start=(kc == 0), stop=(kc == KC - 1))
                            return ins
                        S.op("pe", mm, reads=[f"W{slot}", "Abuf"], writes=[f"q_ps{pb}"])
                        while pending:
                            pending.pop(0)()
                        st = stg[sg]
                        if sec in (0, 1):
                            pv = p[:n, :BWQ].rearrange("t (h x d) -> t h x d", x=2, d=64)
                            sv = st[:n, :BWQ].rearrange("t (h x d) -> t h x d", x=2, d=64)
                            tv = t1[:n, :BWQ].rearrange("t (h x d) -> t h x d", x=2, d=64)
                            cosb = rC[:n, ti, :].rearrange("t (x d) -> t x d", x=2).unsqueeze(1).broadcast_to([n, nh, 2, 64])
                            S.op("dve", lambda e, sv=sv, pv=pv, cosb=cosb: e.tensor_tensor(out=sv, in0=pv, in1=cosb, op=ALU.mult),
                                 reads=[f"q_ps{pb}", "q_rC"], writes=[f"q_stg{sg}"])
                            sin0 = rS[:n, ti, 0:64].unsqueeze(1).broadcast_to([n, nh, 64])
                            sin1 = rS[:n, ti, 64:128].unsqueeze(1).broadcast_to([n, nh, 64])
                            S.op("dve", lambda e, tv=tv, pv=pv, sin0=sin0: e.tensor_tensor(out=tv[:, :, 0, :], in0=pv[:, :, 1, :], in1=sin0, op=ALU.mult),
                                 reads=[f"q_ps{pb}", "q_rS"], writes=["q_t1a"])
                            S.op("dve", lambda e, tv=tv, pv=pv, sin1=sin1: e.tensor_tensor(out=tv[:, :, 1, :], in0=pv[:, :, 0, :], in1=sin1, op=ALU.mult),
                                 reads=[f"q_ps{pb}", "q_rS"], writes=["q_t1b"])
                            S.op("dve", lambda e, st=st, n=n: e.tensor_tensor(out=st[:n, :BWQ], in0=st[:n, :BWQ], in1=t1[:n, :BWQ], op=ALU.add),
                                 reads=[f"q_stg{sg}", "q_t1a", "q_t1b"], writes=[f"q_stg{sg}"])
                        else:
                            S.op("act", lambda e, st=st, n=n, p=p: e.activation(out=st[:n, :BWQ], in_=p[:n, :BWQ], func=AF.Copy),
                                 reads=[f"q_ps{pb}"], writes=[f"q_stg{sg}"])
                        if sec in (1, 2, 4, 5):
                            dst = outs[(kind, sec)][pos0:pos0 + n, jj * BWQ:(jj + 1) * BWQ]
                            S.dma("sp", dmafn(lambda e, st=st, n=n, dst=dst: e.dma_start(out=dst, in_=st[:n, :BWQ]), 1), f"st_qstg{sg}",
                                  reads=[f"q_stg{sg}"], writes=[])
                        bt = b16[bb]
                        if sec in (0, 3):
                            S.op("act", lambda e, bt=bt, st=st, n=n: e.activation(out=bt[:n, :BWQ], in_=st[:n, :BWQ], func=AF.Copy, scale=float(scale)),
                                 reads=[f"q_stg{sg}"], writes=[f"q_b16{bb}"])
                        else:
                            S.op("act", lambda e, bt=bt, st=st, n=n: e.activation(out=bt[:n, :BWQ], in_=st[:n, :BWQ], func=AF.Copy),
                                 reads=[f"q_stg{sg}"], writes=[f"q_b16{bb}"])
                        if sec in (2, 5):
                            c0 = (0 if sec == 2 else DH) + jj * BWQ
                            S.dma("sp", dmafn(lambda e, bt=bt, n=n, tok0=tok0, c0=c0: e.dma_start(out=vbf[tok0:tok0 + n, c0:c0 + BWQ], in_=bt[:n, :BWQ]), 1),
                                  f"st_qb16{bb}", reads=[f"q_b16{bb}"], writes=[])
                        else:
                            def deferred(bt=bt, n=n, bb=bb, sec=sec, jj=jj, tok0=tok0):
                                pt_ = ptb[bb]

                                def tr(e, bt=bt, n=n, pt_=pt_):
                                    ins = None
                                    for h in range(nh):
                                        ins = e.transpose(out=pt_[:, h * 128:h * 128 + n], in_=bt[:n, h * 128:(h + 1) * 128], identity=identb[:n, :n])
                                    return ins
                                S.op("pe", tr, reads=[f"q_b16{bb}", "identb"], writes=[f"q_pt{bb}"])
                                ts_ = tst[bb]
                                S.op("dve", lambda e, ts_=ts_, pt_=pt_, n=n: e.tensor_copy(out=ts_[:, 0:nh, :n], in_=pt_[:, 0:nh * 128].rearrange("p (h t) -> p h t", t=128)[:, :, :n]),
                                     reads=[f"q_pt{bb}"], writes=[f"q_tst{bb}"])
                                isq = sec in (0, 3)
                                hbase = (0 if sec in (0, 1) else NQA) + jj * nh
                                dstT = (qT if isq else kT)[hbase:hbase + nh, :, tok0:tok0 + n].rearrange("h p t -> p h t")
                                S.dma("sp", dmafn(lambda e, ts_=ts_, n=n, dstT=dstT: e.dma_start(out=dstT, in_=ts_[:, 0:nh, :n]), 1),
                                      f"st_qtst{bb}", reads=[f"q_tst{bb}"], writes=[])
                            pending.append(deferred)
                while pending:
                    pending.pop(0)()
                S.emit("qk")

        def phase_attn(group):
            S = Sched(nc)
            tl = tiles_of(group)
            pt_tiles = [t for t in tl if t[0] == "P"]
            s_tiles = [t for t in tl if t[0] == "S"]
            nkp = (max(t[1] for t in pt_tiles) + 128) if pt_tiles else 0
            NKMAX = max(nkp, (PAST + DEC) if s_tiles else 0)
            NKT = (NKMAX + 127) // 128
            NS = 2
            with ExitStack() as ps:
                def t_(name, shape, dt=F32):
                    return ps.enter_context(SBT(name, list(shape), dt))
                pS = [[ps.enter_context(PST(f"a_pS{s}_{i}", [128, 512], F32)) for i in range(2)] for s in range(NS)]
                pT = [ps.enter_context(PST(f"a_pT{s}", [128, 1024], BF16)) for s in range(NS)]
                pOO = [ps.enter_context(PST(f"a_pOO{s}", [128, 512], F32)) for s in range(NS)]
                pO = [pOO[s][:, 0:256] for s in range(NS)]
                pOT = [pOO[s][:, 256:512].bitcast(BF16) for s in range(NS)]
                KTp = [t_(f"a_KTp{i}", [128, max(nkp, 128)], BF16) for i in range(2)]
                QTb = [t_(f"a_QT{i}", [128, GMAX], BF16) for i in range(2)]
                Vp = [t_("a_Vp0", [128, NKT, 256], BF16)]
                if s_tiles:
                    KTs = [t_(f"a_KTs{i}", [128, PAST + DEC], BF16) for i in range(2)]
                    Kc = [t_(f"a_Kc{i}", [128, PAST // 128, 128], BF16) for i in range(2)]
                    Vs = [t_("a_Vs0", [128, PAST // 128 + 1, 256], BF16)]
                E = [[t_(f"a_E{s}_{i}", [128, NKMAX], F32) for i in range(2)] for s in range(NS)]
                Wt = [t_(f"a_W{s}", [128, NKMAX], BF16) for s in range(NS)]
                WT = [t_(f"a_WT{s}", [128, NKT, 128], BF16) for s in range(NS)]
                dg2 = [[t_(f"a_dg{s}_{c}", [128, 128]) for c in range(2)] for s in range(NS)]
                mx = [t_(f"a_mx{s}", [128, 2, 8]) for s in range(NS)]
                sm = [t_(f"a_sm{s}", [128, 2, 8]) for s in range(NS)]
                sc = [t_(f"a_sc{s}", [128, 8]) for s in range(NS)]
                eb = [[t_(f"a_eb{s}_{i}", [128, 512], F32) for i in range(2)] for s in range(NS)]
                ob = [t_(f"a_ob{s}", [128, 256], BF16) for s in range(NS)]
                onec = t_("a_one", [128, 1])
                S.op("dve", lambda e: e.memset(onec[:], 1.0), writes=["a_one"])
                ccnt = [0]

                def attn_tile(st, kindA, QT, KT, Vt, ve, tile, nk_full_tiles, dn, kcol_diag, kc_out):
                    (kind, pos0, n, tok0, acol) = tile
                    nk = nk_full_tiles * 128 + dn
                    nkt = nk_full_tiles + 1
                    chunks = [(c0, min(512, nk_full_tiles * 128 - c0)) for c0 in range(0, nk_full_tiles * 128, 512)]
                    ncomp = len(QT)
                    R = lambda nm: f"{nm}{st}"
                    E0, E1 = E[st]
                    W_, WT_, mx_, sm_, sc_, ob_t = Wt[st], WT[st], mx[st], sm[st], sc[st], ob[st]
                    pcnt = [0]

                    def nextps():
                        i = pcnt[0] % 2; pcnt[0] += 1
                        return pS[st][i], f"a_pS{st}_{i}"
                    nch = len(chunks) + 1
                    if kindA == "diff":
                        ek = [[f"a_E{st}_{c}_{i}" for i in range(nch)] for c in range(ncomp)]
                        for c in range(ncomp):
                            qap, qres = QT[c]; kap, kres = KT[c]
                            Ec = E[st][c]
                            dgc = dg2[st][c]; dgr = f"a_dg{st}_{c}"
                            mk = [f"a_mx{st}_{c}_{i}" for i in range(nch)]
                            sk = [f"a_sm{st}_{c}_{i}" for i in range(nch)]
                            scm = f"a_scm{st}_{c}"; scr = f"a_scr{st}_{c}"
                            p, pr = nextps()
                            S.op("pe", lambda e, p=p, qap=qap, kap=kap: e.matmul(p[:n, :dn], lhsT=qap, rhs=kap[:, kcol_diag:kcol_diag + dn], start=True, stop=True),
                                 reads=[qres, kres], writes=[pr]); yield
                            S.op("dve", lambda e, p=p, dgc=dgc: e.tensor_tensor(out=dgc[:n, :dn], in0=p[:n, :dn], in1=mdiff[:n, :dn], op=ALU.add),
                                 reads=[pr, "mdiff"], writes=[dgr]); yield
                            S.op("dve", lambda e, c=c, dgc=dgc: e.tensor_reduce(out=mx_[:n, c, 0:1], in_=dgc[:n, :dn], axis=AX.X, op=ALU.max), reads=[dgr], writes=[mk[0]]); yield
                            for qi, (c0, w) in enumerate(chunks):
                                p, pr = nextps()
                                S.op("pe", lambda e, p=p, qap=qap, kap=kap, c0=c0, w=w: e.matmul(p[:n, :w], lhsT=qap, rhs=kap[:, c0:c0 + w], start=True, stop=True),
                                     reads=[qres, kres], writes=[pr]); yield
                                S.op("dve", lambda e, p=p, c=c, qi=qi, w=w: e.tensor_reduce(out=mx_[:n, c, qi + 1:qi + 2], in_=p[:n, :w], axis=AX.X, op=ALU.max),
                                     reads=[pr], writes=[mk[qi + 1]]); yield
                            S.op("dve", lambda e, c=c: e.tensor_reduce(out=sc_[:n, c:c + 1], in_=mx_[:n, c, 0:nch], axis=AX.X, op=ALU.max), reads=mk, writes=[scm]); yield
                            S.op("dve", lambda e, c=c: e.tensor_scalar(out=sc_[:n, c:c + 1], in0=sc_[:n, c:c + 1], scalar1=-1.0, scalar2=None, op0=ALU.mult), reads=[scm], writes=[scm]); yield
                            S.op("act", lambda e, c=c, Ec=Ec, dgc=dgc: e.activation(out=Ec[:n, nk_full_tiles * 128:nk], in_=dgc[:n, :dn], func=AF.Exp, bias=sc_[:n, c:c + 1], accum_out=sm_[:n, c, 0:1]),
                                 reads=[dgr, scm], writes=[ek[c][0], sk[0]]); yield
                            for qi, (c0, w) in enumerate(chunks):
                                p, pr = nextps()
                                S.op("pe", lambda e, p=p, qap=qap, kap=kap, c0=c0, w=w: e.matmul(p[:n, :w], lhsT=qap, rhs=kap[:, c0:c0 + w], start=True, stop=True),
                                     reads=[qres, kres], writes=[pr]); yield
                                S.op("act", lambda e, p=p, c=c, qi=qi, c0=c0, w=w, Ec=Ec: e.activation(out=Ec[:n, c0:c0 + w], in_=p[:n, :w], func=AF.Exp, bias=sc_[:n, c:c + 1],
                                                                                                 accum_out=sm_[:n, c, qi + 1:qi + 2]),
                                     reads=[pr, scm], writes=[ek[c][qi + 1], sk[qi + 1]]); yield
                            S.op("dve", lambda e, c=c: e.tensor_reduce(out=sc_[:n, 2 + c:3 + c], in_=sm_[:n, c, 0:nch], axis=AX.X, op=ALU.add), reads=sk, writes=[scr]); yield
                            S.op("dve", lambda e, c=c: e.reciprocal(out=sc_[:n, 2 + c:3 + c], in_=sc_[:n, 2 + c:3 + c]), reads=[scr], writes=[scr]); yield
                        scr0, scr1 = f"a_scr{st}_0", f"a_scr{st}_1"
                        S.op("dve", lambda e: e.tensor_tensor(out=sc_[:n, 3:4], in0=sc_[:n, 3:4], in1=lamt[:n, 0:1], op=ALU.mult), reads=[scr1, "lamt"], writes=[scr1]); yield
                        S.op("dve", lambda e: e.tensor_scalar(out=E1[:n, :nk], in0=E1[:n, :nk], scalar1=sc_[:n, 3:4], scalar2=None, op0=ALU.mult),
                             reads=ek[1] + [scr1], writes=ek[1]); yield
                        S.op("dve", lambda e: e.scalar_tensor_tensor(out=W_[:n, :nk], in0=E0[:n, :nk], scalar=sc_[:n, 2:3], in1=E1[:n, :nk], op0=ALU.mult, op1=ALU.add),
                             reads=ek[0] + ek[1] + [scr0], writes=[R("a_W")]); yield
                    else:
                        qap, qres = QT[0]; kap, kres = KT[0]
                        Lb, NB = E0, E1
                        lk = [f"a_E{st}_0_{i}" for i in range(nch)]
                        nkk = [f"a_E{st}_1_{i}" for i in range(nch)]
                        d0 = nk_full_tiles * 128
                        order = [(-1, d0, dn)] + [(qi, c0, w) for qi, (c0, w) in reversed(list(enumerate(chunks)))]
                        prev_key = None
                        for oi, (qi, c0, w) in enumerate(order):
                            isd = (qi == -1)
                            ki_ = 0 if isd else qi + 1
                            kc0 = kcol_diag if isd else c0
                            p, pr = nextps()
                            ebb = oi % 2
                            ebt = eb[st][ebb]; ebr = f"a_eb{st}_{ebb}"
                            S.op("pe", lambda e, p=p, kc0=kc0, w=w: e.matmul(p[:n, :w], lhsT=qap, rhs=kap[:, kc0:kc0 + w], start=True, stop=True),
                                 reads=[qres, kres], writes=[pr]); yield
                            S.op("act", lambda e, p=p, ebt=ebt, w=w: e.activation(out=ebt[:n, :w], in_=p[:n, :w], func=AF.Exp, scale=-1.0),
                                 reads=[pr], writes=[ebr]); yield
                            S.op("act", lambda e, ebt=ebt, c0=c0, w=w: e.activation(out=Lb[:n, c0:c0 + w], in_=ebt[:n, :w], func=AF.Ln, bias=1.0),
                                 reads=[ebr], writes=[lk[ki_]]); yield
                            if isd:
                                S.op("dve", lambda e, p=p, c0=c0, w=w: e.tensor_tensor(out=NB[:n, c0:c0 + w], in0=p[:n, :w], in1=Lb[:n, c0:c0 + w], op=ALU.add),
                                     reads=[pr, lk[ki_]], writes=[nkk[ki_]]); yield
                                S.op("dve", lambda e, c0=c0, w=w: e.tensor_tensor(out=NB[:n, c0:c0 + w], in0=NB[:n, c0:c0 + w], in1=msb01[:n, :w], op=ALU.mult),
                                     reads=[nkk[ki_], "msb01"], writes=[nkk[ki_]]); yield
                                S.op("dve", lambda e, c0=c0, w=w: e.tensor_tensor_scan(out=NB[:n, c0:c0 + w][:, ::-1], data0=onec[:n, 0:1].broadcast_to([n, w]),
                                                                                     data1=NB[:n, c0:c0 + w][:, ::-1], initial=0.0, op0=ALU.mult, op1=ALU.add),
                                     reads=[nkk[ki_], "a_one"], writes=[nkk[ki_]]); yield
                            else:
                                S.op("dve", lambda e, p=p, c0=c0, w=w: e.tensor_tensor_scan(out=NB[:n, c0:c0 + w][:, ::-1], data0=p[:n, :w][:, ::-1],
                                                                                          data1=Lb[:n, c0:c0 + w][:, ::-1], initial=NB[:n, c0 + w:c0 + w + 1],
                                                                                          op0=ALU.add, op1=ALU.add),
                                     reads=[pr, lk[ki_], prev_key], writes=[nkk[ki_]]); yield
                            prev_key = nkk[ki_]
                        S.op("dve", lambda e: e.tensor_tensor(out=Lb[:n, 0:nk - 1], in0=NB[:n, 1:nk], in1=Lb[:n, 0:nk - 1], op=ALU.add),
                             reads=nkk + lk, writes=lk); yield
                        S.op("dve", lambda e: e.memset(Lb[:n, nk - 1:nk], 0.0), reads=lk, writes=lk); yield
                        S.op("dve", lambda e: e.tensor_tensor(out=Lb[:n, nk - dn:nk], in0=Lb[:n, nk - dn:nk], in1=msbneg[:n, :dn], op=ALU.subtract),
                             reads=lk + ["msbneg"], writes=lk); yield
                        S.op("act", lambda e: e.activation(out=W_[:n, :nk], in_=Lb[:n, :nk], func=AF.Exp, scale=-1.0), reads=lk, writes=[R("a_W")]); yield
                    for k0 in range(0, nkt, 8):
                        kk = min(8, nkt - k0)

                        def tr(e, k0=k0, kk=kk):
                            ins = None
                            for q in range(kk):
                                kt = k0 + q
                                kw = dn if kt == nkt - 1 else 128
                                ins = e.transpose(out=pT[st][:kw, q * 128:q * 128 + n], in_=W_[:n, kt * 128:kt * 128 + kw], identity=identb[:n, :n])
                            return ins
                        S.op("pe", tr, reads=[R("a_W"), "identb"], writes=[R("a_pT")]); yield
                        full = kk if (k0 + kk < nkt) else kk - 1
                        if full > 0:
                            if (k0 // 8) % 2 == 0:
                                S.op("act", lambda e, k0=k0, full=full: e.activation(out=WT_[:, k0:k0 + full, :n], in_=pT[st][:, 0:full * 128].rearrange("p (k t) -> p k t", t=128)[:, :, :n], func=AF.Copy),
                                     reads=[R("a_pT")], writes=[R("a_WT")]); yield
                            else:
                                S.op("dve", lambda e, k0=k0, full=full: e.tensor_copy(out=WT_[:, k0:k0 + full, :n], in_=pT[st][:, 0:full * 128].rearrange("p (k t) -> p k t", t=128)[:, :, :n]),
                                     reads=[R("a_pT")], writes=[R("a_WT")]); yield
                        if full < kk:
                            S.op("act", lambda e, kk=kk: e.activation(out=WT_[:dn, nkt - 1, :n], in_=pT[st][:dn, (kk - 1) * 128:(kk - 1) * 128 + n], func=AF.Copy),
                                 reads=[R("a_pT")], writes=[R("a_WT")]); yield
                    vap, vres = Vt

                    def pv(e):
                        ins = None
                        for kt in range(nkt):
                            kw = dn if kt == nkt - 1 else 128
                            ins = e.matmul(pO[st][:n, :ve], lhsT=WT_[:kw, kt, :n], rhs=vap(kt, kw), start=(kt == 0), stop=(kt == nkt - 1))
                        return ins
                    S.op("pe", pv, reads=[R("a_WT"), vres], writes=[R("a_pO")]); yield
                    if kindA == "diff":
                        S.op("act", lambda e: e.activation(out=eb[st][0][:n, :ve], in_=pO[st][:n, :ve], func=AF.Square, accum_out=sc_[:n, 4:5]),
                             reads=[R("a_pO")], writes=[f"a_eb{st}_0", R("a_sc4")]); yield
                        S.op("act", lambda e: e.activation(out=sc_[:n, 4:5], in_=sc_[:n, 4:5], func=AF.Ln, scale=float(1.0 / ve), bias=float(SUBLN_EPS)), reads=[R("a_sc4")], writes=[R("a_sc4")]); yield
                        S.op("act", lambda e: e.activation(out=sc_[:n, 4:5], in_=sc_[:n, 4:5], func=AF.Exp, scale=-0.5), reads=[R("a_sc4")], writes=[R("a_sc4")]); yield
                        S.op("dve", lambda e: e.scalar_tensor_tensor(out=ob_t[:n, :ve], in0=pO[st][:n, :ve], scalar=sc_[:n, 4:5], in1=sgbc[:n, :ve], op0=ALU.mult, op1=ALU.mult),
                             reads=[R("a_pO"), R("a_sc4"), "sgbc"], writes=[R("a_ob")]); yield
                    else:
                        S.op("act", lambda e: e.activation(out=ob_t[:n, :ve], in_=pO[st][:n, :ve], func=AF.Copy), reads=[R("a_pO")], writes=[R("a_ob")]); yield
                    ne = ve // 128

                    def tro(e):
                        ins = None
                        for q in range(ne):
                            ins = e.transpose(out=pOT[st][:, q * 128:q * 128 + n], in_=ob_t[:n, q * 128:(q + 1) * 128], identity=identb[:n, :n])
                        return ins
                    S.op("pe", tro, reads=[R("a_ob"), "identb"], writes=[R("a_pOT")]); yield
                    S.op("dve", lambda e: e.tensor_copy(out=Abuf[:, kc_out:kc_out + ne, acol:acol + n], in_=pOT[st][:, 0:ne * 128].rearrange("p (k t) -> p k t", t=128)[:, :, :n]),
                         reads=[R("a_pOT")], writes=[f"Abuf_o{st}"]); yield

                def run_items(items):
                    pending = list(items)
                    active = [None] * NS
                    if len(pending) >= 2:
                        S.dry = True
                        nops = sum(1 for _ in pending[0](0))
                        S.dry = False
                        active[0] = pending.pop(0)(0)
                        for _ in range(nops // 2):
                            next(active[0])
                    while True:
                        for s_ in range(NS):
                            if active[s_] is None and pending:
                                active[s_] = pending.pop(0)(s_)
                        if all(a is None for a in active):
                            break
                        for s_ in range(NS):
                            if active[s_] is not None:
                                try:
                                    next(active[s_])
                                except StopIteration:
                                    active[s_] = None

                def run_head(kindA, h):
                    if kindA == "diff":
                        qh = [2 * h, 2 * h + 1]; kh = qh
                        ve = 256; vcol = h * 256; kc_out = 2 * h
                        ck, cvv = ck_d, cv_d
                    else:
                        qh = [NQA + h]; kh = qh
                        ve = 128; vcol = DH + h * 128; kc_out = KC // 2 + h
                        ck, cvv = ck_s, cv_s
                    ncomp = len(qh)
                    for c in range(ncomp):
                        def ldq(e, c=c):
                            res = []
                            for (kind, pos0, nn, tok0, acol0) in chunks_of(group):
                                res.append(e.dma_start(out=QTb[c][:, acol0:acol0 + nn], in_=qT[qh[c], :, tok0:tok0 + nn]))
                            return res
                        S.dma("sp", dmafn(ldq, len(group)), f"ld_QT{c}", writes=[f"a_QT{c}"])
                        if pt_tiles:
                            S.dma("sp", dmafn(lambda e, c=c: e.dma_start(out=KTp[c][:, 0:nkp], in_=kT[kh[c], :, 0:nkp]), 1), f"ld_KTp{c}", writes=[f"a_KTp{c}"])
                    items = []
                    if pt_tiles:
                        def ldvp(e):
                            vv = vbf[0:nkp, vcol:vcol + ve].rearrange("(k p) e -> p k e", p=128)
                            return [e.dma_start(out=Vp[0][:, k0:min(k0 + 8, nkp // 128), 0:ve], in_=vv[:, k0:min(k0 + 8, nkp // 128), :]) for k0 in range(0, nkp // 128, 8)]
                        S.dma("sp", dmafn(ldvp, (nkp // 128 + 7) // 8), "ld_Vp", writes=["a_Vp0"])
                        for tile in pt_tiles:
                            (kind, pos0, n, tok0, acol) = tile
                            nfull = pos0 // 128
                            items.append(lambda st, tile=tile, n=n, acol=acol, nfull=nfull, pos0=pos0: attn_tile(
                                st, kindA, [(QTb[c][:, acol:acol + n], f"a_QT{c}") for c in range(ncomp)],
                                [(KTp[c], f"a_KTp{c}") for c in range(ncomp)],
                                (lambda kt, kw: Vp[0][:kw, kt, 0:ve], "a_Vp0"), ve, tile, nfull, 128, pos0, kc_out))
                    if s_tiles:
                        tile = s_tiles[0]
                        (kind, pos0, n, tok0, acol) = tile
                        npast = PAST // 128
                        for c in range(ncomp):
                            kcol = (kh[c] if kindA == "diff" else h) * 128
                            S.dma("pool", dmafn(lambda e, c=c, kcol=kcol: e.dma_start(out=Kc[c][:, :, :], in_=ck[:, kcol:kcol + 128].rearrange("(k p) d -> p k d", p=128)), 1),
                                  f"ld_Kc{c}", writes=[f"a_Kc{c}"])
                            for k0 in range(0, npast, 8):
                                kk = min(8, npast - k0)
                                tb = ccnt[0] % NS; ccnt[0] += 1

                                def trk(e, c=c, k0=k0, kk=kk, tb=tb):
                                    ins = None
                                    for q in range(kk):
                                        ins = e.transpose(out=pT[tb][:, q * 128:(q + 1) * 128], in_=Kc[c][:, k0 + q, :], identity=identb[:, :])
                                    return ins
                                S.op("pe", trk, reads=[f"a_Kc{c}", "identb"], writes=[f"a_pT{tb}"])
                                S.op("act", lambda e, c=c, k0=k0, kk=kk, tb=tb: e.activation(out=KTs[c][:, k0 * 128:(k0 + kk) * 128], in_=pT[tb][:, 0:kk * 128], func=AF.Copy),
                                     reads=[f"a_pT{tb}"], writes=[f"a_KTs{c}"])
                            S.dma("sp", dmafn(lambda e, c=c: e.dma_start(out=KTs[c][:, PAST:PAST + n], in_=kT[kh[c], :, tok0:tok0 + n]), 1), f"ld_KTs{c}",
                                  writes=[f"a_KTs{c}"])
                        vc0 = vcol if kindA == "diff" else h * 128
                        S.dma("pool", dmafn(lambda e, vc0=vc0: e.dma_start(out=Vs[0][:, 0:npast, 0:ve], in_=cvv[:, vc0:vc0 + ve].rearrange("(k p) e -> p k e", p=128)), 1),
                              "ld_Vs", writes=["a_Vs0"])
                        S.dma("sp", dmafn(lambda e: e.dma_start(out=Vs[0][:n, npast, 0:ve], in_=vbf[tok0:tok0 + n, vcol:vcol + ve]), 1), "ld_Vs2", writes=["a_Vs0"])
                        items.append(lambda st: attn_tile(
                            st, kindA, [(QTb[c][:, acol:acol + n], f"a_QT{c}") for c in range(ncomp)],
                            [(KTs[c], f"a_KTs{c}") for c in range(ncomp)],
                            (lambda kt, kw: Vs[0][:kw, kt, 0:ve], "a_Vs0"), ve, tile, npast, n, PAST, kc_out))
                    run_items(items[::-1])

                for h in range(HA):
                    run_head("diff", h)
                for h in range(HB):
                    run_head("sb", h)
                S.emit("at")

        def phase_proj_resid(group, wmat, l, name):
            S = Sched(nc)
            with ExitStack() as ps:
                alloc_W(ps); pst = [ps.enter_context(PST(f"ps{i}", [128, 512], F32)) for i in range(6)]
                stg = [ps.enter_context(SBT(f"stg{i}", [128, 512], F32)) for i in range(3)]
                dense_ws(S, group, D // 512, lambda j: wsrc(wmat, 0, j * 512, 512), gated_residual_epilogue(stg, l, 0), pst)
                S.emit(name)

        def gcols(group):
            return [(c[3], c[2], c[4]) for c in chunks_of(group)]

        def phase_mlp(group, l):
            S = Sched(nc)
            nch_ = len(chunks_of(group))
            with ExitStack() as ps:
                alloc_W(ps); pst = [ps.enter_context(PST(f"ps{i}", [128, 512], F32)) for i in range(6)]
                rl = [ps.enter_context(SBT(f"u_rl{i}", [128, 512], F32)) for i in range(2)]
                hb = [ps.enter_context(SBT(f"u_hb{i}", [128, 512], BF16)) for i in range(3)]
                stg = [ps.enter_context(SBT(f"stg{i}", [128, 512], F32)) for i in range(3)]
                cnt = [0]

                def ep(S_, j, s, ci, ch, p, pname):
                    (kind, pos0, n, tok0, acol) = ch
                    ko = j * 4 + s
                    r = cnt[0] % 2; b = cnt[0] % 3; cnt[0] += 1
                    S_.op("act", lambda e, r=r, n=n, p=p: e.activation(out=rl[r][:, :n], in_=p[:, :n], func=AF.Relu), reads=[pname], writes=[f"u_rl{r}"])
                    S_.op("dve", lambda e, r=r, b=b, n=n: e.tensor_tensor(out=hb[b][:, :n], in0=rl[r][:, :n], in1=rl[r][:, :n], op=ALU.mult), reads=[f"u_rl{r}"], writes=[f"u_hb{b}"])
                    S_.dma("sp", dmafn(lambda e, b=b, n=n, ko=ko, tok0=tok0: e.dma_start(out=gTv[:, ko, tok0:tok0 + n], in_=hb[b][:, :n]), 1), f"st_uhb{b}",
                           reads=[f"u_hb{b}"], writes=[f"gT_{ko}_{ci}"])
                dense_ws(S, group, DFF // 512, lambda j: wsrc(m_up[l], 0, j * 512, 512), ep, pst)
                for s4 in range(DFF // D):
                    def lda(e, s4=s4):
                        res = []
                        for (tok0, n, acol) in gcols(group):
                            for k0 in range(0, KC, KS):
                                res.append(e.dma_start(out=Abuf[:, k0:k0 + KS, acol:acol + n], in_=gTv[:, s4 * KC + k0:s4 * KC + k0 + KS, tok0:tok0 + n]))
                        return res
                    S.dma("sp", dmafn(lda, len(group) * (KC // KS)), "ld_A",
                          reads=[f"gT_{ko}_{ci}" for ko in range(s4 * KC, (s4 + 1) * KC) for ci in range(nch_)], writes=["Abuf"])
                    dense_ws(S, group, D // 512, lambda j, s4=s4: wsrc(m_dn[l], s4 * D, j * 512, 512), gated_residual_epilogue(stg, l, 1), pst)
                S.emit("ml")

        def phase_conv(group, last_group):
            S = Sched(nc)
            with ExitStack() as ps:
                alloc_W(ps); pst = [ps.enter_context(PST(f"ps{i}", [128, 512], F32)) for i in range(6)]
                sgm = [ps.enter_context(SBT(f"g_sg{i}", [128, 512], F32)) for i in range(2)]
                ust = [ps.enter_context(SBT(f"g_u{i}", [128, 512], F32)) for i in range(3)]
                cnt = [0]

                def ep(S_, j, s, ci, ch, p, pname):
                    (kind, pos0, n, tok0, acol) = ch
                    ko = j * 2 + s
                    r = cnt[0] % 2; b = cnt[0] % 3; cnt[0] += 1
                    S_.op("act", lambda e, r=r, n=n, p=p: e.activation(out=sgm[r][:, :n], in_=p[1][:, :n], func=AF.Sigmoid), reads=[pname[1]], writes=[f"g_sg{r}"])
                    S_.op("dve", lambda e, r=r, b=b, n=n, p=p: e.tensor_tensor(out=ust[b][:, :n], in0=p[0][:, :n], in1=sgm[r][:, :n], op=ALU.mult),
                          reads=[pname[0], f"g_sg{r}"], writes=[f"g_u{b}"])
                    dst = (uTpv[:, ko, CW - 1 + pos0:CW - 1 + pos0 + n] if kind == "P" else uTsv[:, ko, CW - 1:CW - 1 + n])
                    S_.dma("sp", dmafn(lambda e, b=b, n=n, dst=dst: e.dma_start(out=dst, in_=ust[b][:, :n]), 1), f"st_gu{b}", reads=[f"g_u{b}"], writes=[])

                def src(j):
                    va = pw1[:, j * 256:(j + 1) * 256].rearrange("(k p) n -> p k n", p=128)
                    vb_ = pw1[:, D + j * 256:D + (j + 1) * 256].rearrange("(k p) n -> p k n", p=128)
                    return (va, vb_)

                def load_pair(j):
                    slot = wstate["n"] % 2
                    wstate["n"] += 1
                    va, vb_ = src(j)
                    hs = max(1, KC // 4)

                    def fn(e, slot=slot):
                        res = []
                        for (v, c0) in ((va, 0), (vb_, 256)):
                            for q in range(0, KC, hs):
                                res.append(e.dma_start(out=Wb[slot][:, q:q + hs, c0:c0 + 256], in_=v[:, q:q + hs, :]))
                        return res
                    S.dma("pool", dmafn(fn, 2 * (KC // hs)), f"ldW{slot}", writes=[f"W{slot}"])
                    return slot
                chunks = chunks_of(group)
                nblk = D // 256
                slots = {0: load_pair(0)}
                c2 = 0
                for j in range(nblk):
                    if j + 1 < nblk:
                        slots[j + 1] = load_pair(j + 1)
                    slot = slots[j]
                    for s in range(2):
                        for ci, ch in enumerate(chunks):
                            (kind, pos0, n, tok0, acol) = ch
                            ba = (c2 * 2) % 6; bb = (c2 * 2 + 1) % 6
                            c2 += 1

                            def mm(e, slot=slot, s=s, ba=ba, bb=bb, n=n, acol=acol):
                                ins = None
                                for kc in range(KC):
                                    ins = e.matmul(pst[ba][:, :n], lhsT=Wb[slot][:, kc, s * 128:(s + 1) * 128], rhs=Abuf[:, kc, acol:acol + n], start=(kc == 0), stop=(kc == KC - 1))
                                for kc in range(KC):
                                    ins = e.matmul(pst[bb][:, :n], lhsT=Wb[slot][:, kc, 256 + s * 128:256 + (s + 1) * 128], rhs=Abuf[:, kc, acol:acol + n], start=(kc == 0), stop=(kc == KC - 1))
                                return ins
                            S.op("pe", mm, reads=[f"W{slot}", "Abuf"], writes=[f"ps{ba}", f"ps{bb}"])
                            ep(S, j, s, ci, ch, (pst[ba], pst[bb]), (f"ps{ba}", f"ps{bb}"))
                S.emit("p1")
            S = Sched(nc)
            chunks = chunks_of(group)
            ntok = sum(c[2] for c in chunks)
            segs = []
            for (kind, pos0, n, tok0, acol) in chunks:
                if segs and segs[-1][0] == kind and segs[-1][1] + segs[-1][2] == pos0:
                    segs[-1][2] += n
                else:
                    segs.append([kind, pos0, n])
            LU = sum(sg[2] + CW - 1 for sg in segs)
            LY = LU - (CW - 1)
            ycol = []
            for (kind, pos0, n, tok0, acol) in chunks:
                o = 0
                for sg in segs:
                    if sg[0] == kind and sg[1] <= pos0 < sg[1] + sg[2]:
                        ycol.append(o + pos0 - sg[1])
                        break
                    o += sg[2] + CW - 1
            NW = 4
            ochunks = [(c0, min(512, LY - c0)) for c0 in range(0, LY, 512)]
            with ExitStack() as ps:
                ub = [ps.enter_context(SBT(f"d_u{i}", [128, LU], BF16)) for i in range(NW)]
                dgw = [ps.enter_context(SBT(f"d_g{i}", [128, CW, 128], BF16)) for i in range(NW)]
                yb = [ps.enter_context(SBT(f"d_y{i}", [128, LU], F32)) for i in range(NW)]
                ysq = [ps.enter_context(SBT(f"d_q{i}", [128, LU], F32)) for i in range(NW)]
                pst = [ps.enter_context(PST(f"d_ps{i}", [128, 512], F32)) for i in range(2 * len(chunks))]
                pcv = [ps.enter_context(PST(f"d_pc{i}", [128, 512], F32)) for i in range(2)]
                mean = ps.enter_context(SBT("d_mean", [128, ntok], F32))
                rstd = ps.enter_context(SBT("d_rstd", [128, ntok], F32))
                ecnt = [0]
                for k0 in range(0, KC, NW):
                    wave = list(range(k0, min(KC, k0 + NW)))
                    for b, kc in enumerate(wave):
                        def ldu(e, b=b, kc=kc):
                            res = []
                            o = 0
                            for (kind, pos0, n) in segs:
                                srcv = uTpv[:, kc, pos0:pos0 + n + CW - 1] if kind == "P" else uTsv[:, kc, 0:n + CW - 1]
                                res.append(e.dma_start(out=ub[b][:, o:o + n + CW - 1], in_=srcv))
                                o += n + CW - 1
                            return res
                        S.dma("pool", dmafn(ldu, len(segs)), f"ld_du{b}", writes=[f"d_u{b}"])
                        S.op("dve", lambda e, b=b, kc=kc: e.tensor_tensor(out=dgw[b][:, :, :], in0=identb[:, :].unsqueeze(1).broadcast_to([128, CW, 128]),
                                                                         in1=dwT[:, kc, :].unsqueeze(2).broadcast_to([128, CW, 128]), op=ALU.mult),
                             reads=["identb"], writes=[f"d_g{b}"])
                    for b, kc in enumerate(wave):
                        for (c0, wd) in ochunks:
                            pb = ecnt[0] % 2; ecnt[0] += 1

                            def cv(e, b=b, c0=c0, wd=wd, pb=pb):
                                ins = None
                                for w in range(CW):
                                    ins = e.matmul(pcv[pb][:, :wd], lhsT=dgw[b][:, w, :], rhs=ub[b][:, c0 + w:c0 + w + wd], start=(w == 0), stop=(w == CW - 1))
                                return ins
                            S.op("pe", cv, reads=[f"d_g{b}", f"d_u{b}"], writes=[f"d_pc{pb}"])
                            if ecnt[0] % 2 == 0:
                                S.op("act", lambda e, b=b, kc=kc, c0=c0, wd=wd, pb=pb: e.activation(out=yb[b][:, c0:c0 + wd], in_=pcv[pb][:, :wd], func=AF.Identity, bias=dwbT[:, kc:kc + 1]),
                                     reads=[f"d_pc{pb}"], writes=[f"d_y{b}_{c0}"])
                            else:
                                S.op("dve", lambda e, b=b, kc=kc, c0=c0, wd=wd, pb=pb: e.tensor_scalar(out=yb[b][:, c0:c0 + wd], in0=pcv[pb][:, :wd], scalar1=dwbT[:, kc:kc + 1], scalar2=None, op0=ALU.add),
                                     reads=[f"d_pc{pb}"], writes=[f"d_y{b}_{c0}"])
                    for b, kc in enumerate(wave):
                        ykeys = [f"d_y{b}_{c0}" for (c0, wd) in ochunks]
                        S.op("act", lambda e, b=b: e.activation(out=ysq[b][:, 0:LY], in_=yb[b][:, 0:LY], func=AF.Square), reads=ykeys, writes=[f"d_q{b}"])

                        def st(e, b=b, kc=kc):
                            ins = None
                            for ci, (kind, pos0, n, tok0, acol) in enumerate(chunks):
                                ins = e.matmul(pst[2 * ci][:, :n], lhsT=onesf[:, :], rhs=yb[b][:, ycol[ci]:ycol[ci] + n], start=(kc == 0), stop=(kc == KC - 1))
                                ins = e.matmul(pst[2 * ci + 1][:, :n], lhsT=onesf[:, :], rhs=ysq[b][:, ycol[ci]:ycol[ci] + n], start=(kc == 0), stop=(kc == KC - 1))
                            return ins
                        S.op("pe", st, reads=ykeys + [f"d_q{b}", "onesf"], writes=["d_psall"])

                        def sty(e, b=b, kc=kc):
                            res = []
                            for ci, (kind, pos0, n, tok0, acol) in enumerate(chunks):
                                res.append(e.dma_start(out=yTv[:, kc, tok0:tok0 + n], in_=yb[b][:, ycol[ci]:ycol[ci] + n]))
                            return res
                        S.dma("sp", dmafn(sty, len(chunks)), f"st_dy{b}", reads=ykeys, writes=[f"yT{kc}"])
                for ci, (kind, pos0, n, tok0, acol) in enumerate(chunks):
                    S.op("dve", lambda e, ci=ci, n=n, acol=acol: e.tensor_scalar(out=mean[:, acol:acol + n], in0=pst[2 * ci][:, :n], scalar1=float(1.0 / D), scalar2=None, op0=ALU.mult),
                         reads=["d_psall"], writes=["d_mean"])
                    S.op("dve", lambda e, ci=ci, n=n, acol=acol: e.tensor_scalar(out=rstd[:, acol:acol + n], in0=pst[2 * ci + 1][:, :n], scalar1=float(1.0 / D), scalar2=None, op0=ALU.mult),
                         reads=["d_psall"], writes=["d_rstd"])
                S.op("dve", lambda e: e.tensor_tensor(out=ysq[0][:, :ntok], in0=mean[:, :ntok], in1=mean[:, :ntok], op=ALU.mult), reads=["d_mean", "d_q0"], writes=["d_q0"])
                S.op("dve", lambda e: e.tensor_tensor(out=rstd[:, :ntok], in0=rstd[:, :ntok], in1=ysq[0][:, :ntok], op=ALU.subtract), reads=["d_rstd", "d_q0"], writes=["d_rstd"])
                S.op("act", lambda e: e.activation(out=rstd[:, :ntok], in_=rstd[:, :ntok], func=AF.Sqrt, bias=float(NORM_EPS)), reads=["d_rstd"], writes=["d_rstd"])
                S.op("dve", lambda e: e.reciprocal(out=rstd[:, :ntok], in_=rstd[:, :ntok]), reads=["d_rstd"], writes=["d_rstd"])
                for kc in range(KC):
                    b = kc % NW

                    def ldy(e, b=b, kc=kc):
                        res = []
                        for (kind, pos0, n, tok0, acol) in chunks:
                            res.append(e.dma_start(out=yb[b][:, acol:acol + n], in_=yTv[:, kc, tok0:tok0 + n]))
                        return res
                    S.dma("sp", dmafn(ldy, len(chunks)), f"ld_dy{b}", reads=[f"yT{kc}"], writes=[f"d_y{b}"] + [f"d_y{b}_{c0}" for (c0, wd) in ochunks])
                    S.op("dve", lambda e, b=b: e.tensor_tensor(out=yb[b][:, :ntok], in0=yb[b][:, :ntok], in1=mean[:, :ntok], op=ALU.subtract), reads=[f"d_y{b}", "d_mean"], writes=[f"d_y{b}"])
                    S.op("dve", lambda e, b=b: e.tensor_tensor(out=yb[b][:, :ntok], in0=yb[b][:, :ntok], in1=rstd[:, :ntok], op=ALU.mult), reads=[f"d_y{b}", "d_rstd"], writes=[f"d_y{b}"])
                    S.op("act", lambda e, b=b, kc=kc: e.activation(out=Abuf[:, kc, 0:ntok], in_=yb[b][:, :ntok], func=AF.Silu, scale=clngT[:, kc:kc + 1], bias=clnbT[:, kc:kc + 1]),
                         reads=[f"d_y{b}"], writes=["Abuf"])
                S.emit("dw")

        def phase_conv_state_init():
            S = Sched(nc)
            with ExitStack() as ps:
                z = ps.enter_context(SBT("z_z", [128, KC, CW - 1], F32))
                sti = ps.enter_context(SBT("z_in", [CW - 1, D], F32))
                stt = ps.enter_context(SBT("z_t", [128, KC, CW - 1], F32))
                pz = ps.enter_context(PST("z_ps", [128, 512], F32))
                S.op("dve", lambda e: e.memset(z[:], 0.0), writes=["z_z"])
                S.dma("sp", dmafn(lambda e: [e.dma_start(out=uTpv[:, k0:k0 + KS, 0:CW - 1], in_=z[:, k0:k0 + KS, :]) for k0 in range(0, KC, KS)], KC // KS), "st_z", reads=["z_z"], writes=[])
                S.dma("sp", dmafn(lambda e: e.dma_start(out=sti[:], in_=st_c[:, :]), 1), "ld_zi", writes=["z_in"])
                for kc in range(KC):
                    S.op("pe", lambda e, kc=kc: e.transpose(out=pz[:, 0:CW - 1], in_=sti[:, kc * 128:(kc + 1) * 128], identity=ident[:CW - 1, :CW - 1]), reads=["z_in", "ident"], writes=["z_ps"])
                    S.op("dve", lambda e, kc=kc: e.tensor_copy(out=stt[:, kc, :], in_=pz[:, 0:CW - 1]), reads=["z_ps"], writes=["z_t"])
                S.dma("sp", dmafn(lambda e: [e.dma_start(out=uTsv[:, k0:k0 + KS, 0:CW - 1], in_=stt[:, k0:k0 + KS, :]) for k0 in range(0, KC, KS)], KC // KS), "st_zt", reads=["z_t"], writes=[])
                S.emit("zi")

        def phase_conv_state_out():
            S = Sched(nc)
            with ExitStack() as ps:
                ui = ps.enter_context(SBT("o_in", [128, KC, CW - 1], F32))
                uo = ps.enter_context(SBT("o_out", [CW - 1, D], F32))
                po = ps.enter_context(PST("o_ps", [128, 512], F32))
                for (srcv, dst) in ((uTpv[:, :, SEQ:SEQ + CW - 1], cv_p), (uTsv[:, :, DEC:DEC + CW - 1], cv_so)):
                    S.dma("sp", dmafn(lambda e, srcv=srcv: [e.dma_start(out=ui[:, k0:k0 + KS, :], in_=srcv[:, k0:k0 + KS, :]) for k0 in range(0, KC, KS)], KC // KS), "ld_oi", writes=["o_in"])
                    for kc in range(KC):
                        S.op("pe", lambda e, kc=kc: e.transpose(out=po[:CW - 1, 0:128], in_=ui[:, kc, :], identity=ident[:, :]), reads=["o_in", "ident"], writes=["o_ps"])
                        S.op("dve", lambda e, kc=kc: e.tensor_copy(out=uo[:, kc * 128:(kc + 1) * 128], in_=po[:CW - 1, 0:128]), reads=["o_ps"], writes=["o_out"])
                    S.dma("sp", dmafn(lambda e, dst=dst: e.dma_start(out=dst[:, :], in_=uo[:]), 1), "st_oo", reads=["o_out"], writes=[])
                S.emit("zo")

        stop_after = cfg.get("stop_after", 99)
        phase_consts()
        phase_mod()
        phase_xin()
        if stop_after >= 2:
            phase_conv_state_init()
            for gi, g in enumerate(groups):
                phase_norm(g, 0, 0)
                phase_qkv(g)
                if stop_after >= 2.5:
                    phase_attn(g)
                if stop_after >= 3:
                    phase_proj_resid(g, w_out, 0, "wo")
                if stop_after >= 4:
                    phase_norm(g, 0, 1)
                    phase_mlp(g, 0)
                if stop_after >= 5:
                    phase_norm(g, 1, 0)
                    phase_conv(g, gi == len(groups) - 1)
                    phase_proj_resid(g, pw2, 1, "p2")
                if stop_after >= 6:
                    phase_norm(g, 1, 1)
                    phase_mlp(g, 1)
                    phase_norm(g, 0, 0, final=True)
            if stop_after >= 5:
                phase_conv_state_out()
    return nc


def make_consts(cfg):
    D, SEQ, DEC, PAST = cfg["D"], cfg["SEQ"], cfg["DEC"], cfg["PAST"]
    half = 64
    inv = np.power(np.float32(10000.0), -np.arange(half, dtype=np.float32) / np.float32(half)).astype(np.float32)
    pos = np.concatenate([np.arange(SEQ), PAST + np.arange(DEC)]).astype(np.float32)
    ang = (pos[:, None] * inv[None, :]).astype(np.float32)
    cos = np.cos(ang).astype(np.float32); sin = np.sin(ang).astype(np.float32)
    q = np.arange(128)[:, None]; k = np.arange(128)[None, :]
    return dict(
        k_id=np.eye(128, dtype=np.float32),
        k_ropeC=np.concatenate([cos, cos], axis=1), k_ropeS=np.concatenate([-sin, sin], axis=1),
        k_mdiff=np.where((k // 64) <= (q // 64), 0.0, -1e30).astype(np.float32),
        k_msb01=(k < q).astype(np.float32),
        k_msbneg=np.where(k < q, 0.0, -1e30).astype(np.float32),
    )


def make_in_maps(cfg, inp, ncores):
    D, SEQ, DEC, PAST = cfg["D"], cfg["SEQ"], cfg["DEC"], cfg["PAST"]
    KC = D // 128
    DH = D // 2
    f = lambda a: np.ascontiguousarray(np.asarray(a, dtype=np.float32))
    cst = make_consts(cfg)
    shared = dict(
        w_mod=f(inp["w_mod"]), b_mod=f(inp["b_mod"]).reshape(2, 6 * KC, 128),
        n_mix=f(inp["norm_mix"]).reshape(2, KC, 128), n_mlp=f(inp["norm_mlp"]).reshape(2, KC, 128),
        w_in=f(inp["w_attn_in"])[0], w_out=f(inp["w_attn_out"])[0],
        lam_v=np.stack([f(inp["lambda_q1"])[0], f(inp["lambda_k1"])[0], f(inp["lambda_q2"])[0], f(inp["lambda_k2"])[0]]),
        subln=f(inp["diff_subln_g"]).reshape(1, 256),
        pw1=f(inp["conv_pw1"])[0], cdw=f(inp["conv_dw"])[0], cdw_b=f(inp["conv_dw_b"]).reshape(KC, 128),
        cln_g=f(inp["conv_ln_g"]).reshape(KC, 128), cln_b=f(inp["conv_ln_b"]).reshape(KC, 128), pw2=f(inp["conv_pw2"])[0],
        m_up=f(inp["mlp_up"]), m_dn=f(inp["mlp_down"]), fin_g=f(inp["final_g"]).reshape(KC, 128),
        **cst,
    )
    maps = []
    for b in range(ncores):
        m = dict(shared)
        m.update(
            x_p=f(inp["x_prompt"][b]), x_s=f(inp["x_sample"][b]),
            c_p=f(inp["c_prompt"][b]).reshape(KC, 128), c_s=f(inp["c_sample"][b]).reshape(KC, 128),
            ck_d=f(inp["cache_k_diff"][0, b]).reshape(PAST, -1), cv_d=f(inp["cache_v_diff"][0, b]).reshape(PAST, -1),
            ck_s=f(inp["cache_k_sb"][0, b]).reshape(PAST, -1), cv_s=f(inp["cache_v_sb"][0, b]).reshape(PAST, -1),
            st_c=f(inp["state_conv"][0, b]),
        )
        maps.append(m)
    return maps


def assemble(cfg, results):
    D, SEQ, DEC = cfg["D"], cfg["SEQ"], cfg["DEC"]
    HA = D // 512; HB = D // 256
    st = lambda k: np.stack([np.asarray(r[k], dtype=np.float32) for r in results])
    B = len(results)
    return (
        st("y_p"), st("y_s"),
        st("kd_p").reshape(1, B, SEQ, 2 * HA, 128), st("vd_p").reshape(1, B, SEQ, HA, 256),
        st("ks_p").reshape(1, B, SEQ, HB, 128), st("vs_p").reshape(1, B, SEQ, HB, 128),
        st("cv_p").reshape(1, B, CW - 1, D),
        st("kd_s").reshape(1, B, DEC, 2 * HA, 128), st("vd_s").reshape(1, B, DEC, HA, 256),
        st("ks_s").reshape(1, B, DEC, HB, 128), st("vs_s").reshape(1, B, DEC, HB, 128),
        st("cv_s_o").reshape(1, B, CW - 1, D),
    )


def kernel(**inputs):
    cfg = FULL
    nc = build(cfg)
    maps = make_in_maps(cfg, inputs, 8)
    res = run_bass_kernel_spmd(nc, maps, core_ids=list(range(8)))
    return assemble(cfg, res.results)
```
